# Optimizing a Trainium2 kernel written in Bass

```python
import jax, jax.numpy as jnp
from jax import lax
import numpy as np

D_MODEL = 1024
BATCH = 4
SEQ = 4096
DEPTH = 1

CHUNK = 64
Q_BLOCK = 128
SB_HEADS = 8
SB_HEAD_DIM = 64
SB_WIDTH = SB_HEADS * SB_HEAD_DIM
HG_HEADS = 4
HG_HEAD_DIM = 128
HG_WIDTH = HG_HEADS * HG_HEAD_DIM
N_BRANCH = 2
D_FF = 4 * D_MODEL
IN_COLS = 3 * SB_WIDTH + 4 * HG_WIDTH + N_BRANCH * D_MODEL
EPS = 1e-6

kernel_name = "hybrid_stickbreak_hgrn2_gated_merge"


def rmsnorm(x, g):
    xf = x.astype(jnp.float32)
    y = xf * lax.rsqrt(jnp.mean(xf * xf, axis=-1, keepdims=True) + EPS)
    return y * g.astype(jnp.float32)


def split_heads(t, n_heads):
    b, s, w = t.shape
    return t.reshape(b, s, n_heads, w // n_heads).transpose(0, 2, 1, 3)


def merge_heads(t):
    b, h, s, d = t.shape
    return t.transpose(0, 2, 1, 3).reshape(b, s, h * d)


def stick_breaking_attention(q, k, v):
    b, h, s, dh = q.shape
    nb = s // Q_BLOCK
    scale = dh ** -0.5
    qb = q.reshape(b, h, nb, Q_BLOCK, dh).transpose(2, 0, 1, 3, 4)
    kpos = jnp.arange(s)

    def block(args):
        qi, i = args
        qpos = i * Q_BLOCK + jnp.arange(Q_BLOCK)
        z = jnp.einsum('bhqd,bhkd->bhqk', qi, k) * scale
        mask = kpos[None, :] < qpos[:, None]
        log_beta = jax.nn.log_sigmoid(z)
        log_rem = jnp.where(mask, log_beta - z, 0.0)
        between = lax.cumsum(log_rem, axis=3, reverse=True) - log_rem
        w = jnp.where(mask, jnp.exp(log_beta + between), 0.0)
        return jnp.einsum('bhqk,bhkd->bhqd', w, v)

    out = lax.map(block, (qb, jnp.arange(nb)))
    return out.transpose(1, 2, 0, 3, 4).reshape(b, h, s, dh)


def hgrn2_chunkwise(q, k, v, log_f):
    b, h, s, dk = q.shape
    dv = v.shape[-1]
    nc = s // CHUNK

    def to_chunks(t):
        return t.reshape(b, h, nc, CHUNK, t.shape[-1]).transpose(2, 0, 1, 3, 4)

    qc, kc, vc, gc = to_chunks(q), to_chunks(k), to_chunks(v), to_chunks(log_f)
    causal = jnp.arange(CHUNK)[:, None] >= jnp.arange(CHUNK)[None, :]

    def step(state, inp):
        qi, ki, vi, gi = inp
        bcum = jnp.cumsum(gi, axis=2)
        diff = bcum[:, :, :, None, :] - bcum[:, :, None, :, :]
        decay = jnp.exp(jnp.where(causal[:, :, None], diff, -jnp.inf))
        scores = jnp.einsum('bhtd,bhsd,bhtsd->bhts', qi, ki, decay)
        o = jnp.einsum('bhts,bhse->bhte', scores, vi) \
            + jnp.einsum('bhtd,bhde->bhte', qi * jnp.exp(bcum), state)
        b_last = bcum[:, :, -1:, :]
        k_dec = ki * jnp.exp(b_last - bcum)
        state = jnp.exp(b_last[:, :, 0, :])[..., None] * state \
            + jnp.einsum('bhsd,bhse->bhde', k_dec, vi)
        return state, o

    s0 = jnp.zeros((b, h, dk, dv), jnp.float32)
    _, o = lax.scan(step, s0, (qc, kc, vc, gc))
    return o.transpose(1, 2, 0, 3, 4).reshape(b, h, s, dv)


def hybrid_layer(x, layer, norm1_g, w_in, b_gate, lb_logits, hg_norm_g, w_o_sb, w_o_hg, w_out,
                 norm2_g, w_ff1, w_ff2):
    f32 = jnp.float32
    xn = rmsnorm(x, norm1_g[layer])
    u = xn @ w_in[layer].astype(f32)
    c = np.cumsum([SB_WIDTH, SB_WIDTH, SB_WIDTH, HG_WIDTH, HG_WIDTH, HG_WIDTH, HG_WIDTH, D_MODEL])
    q_sb, k_sb, v_sb, f_raw, i_hg, q_hg, g_hg, gate_sb, gate_hg = jnp.split(u, list(c), axis=-1)

    o_sb = stick_breaking_attention(split_heads(q_sb, SB_HEADS), split_heads(k_sb, SB_HEADS),
                                    split_heads(v_sb, SB_HEADS))
    o_sb = merge_heads(o_sb)

    lb_all = jnp.cumsum(jax.nn.softmax(lb_logits.astype(f32), axis=0), axis=0)
    lb = lb_all[layer]
    sig = jax.nn.sigmoid(f_raw)
    f = lb + (1.0 - lb) * sig
    log_f = jnp.log(f)
    k_hg = (1.0 - lb) * jax.nn.sigmoid(-f_raw)
    o_hg = hgrn2_chunkwise(split_heads(jax.nn.silu(q_hg), HG_HEADS), split_heads(k_hg, HG_HEADS),
                           split_heads(i_hg, HG_HEADS), split_heads(log_f, HG_HEADS))
    o_hg = rmsnorm(o_hg, hg_norm_g[layer].reshape(HG_HEADS, 1, HG_HEAD_DIM))
    o_hg = merge_heads(o_hg) * jax.nn.silu(g_hg)

    bg = b_gate[layer].astype(f32)
    g_a = jax.nn.sigmoid(gate_sb + bg[:D_MODEL])
    g_b = jax.nn.sigmoid(gate_hg + bg[D_MODEL:])
    merged = g_a * (o_sb @ w_o_sb[layer].astype(f32)) + g_b * (o_hg @ w_o_hg[layer].astype(f32))
    h = x.astype(f32) + merged @ w_out[layer].astype(f32)

    hn = rmsnorm(h, norm2_g[layer])
    a = jnp.square(jax.nn.relu(hn @ w_ff1[layer].astype(f32)))
    h = h + a @ w_ff2[layer].astype(f32)
    return h


def setup_inputs(seed: int = 0) -> dict:
    key = jax.random.key(seed)
    ks = jax.random.split(key, 14)
    f32 = jnp.float32

    def nrm(k, shape, fan_in):
        return jax.random.normal(k, shape, f32) * (fan_in ** -0.5)

    return {
        "x": jax.random.normal(ks[0], (BATCH, SEQ, D_MODEL), f32),
        "norm1_g": 1.0 + 0.05 * jax.random.normal(ks[1], (DEPTH, D_MODEL), f32),
        "w_in": nrm(ks[2], (DEPTH, D_MODEL, IN_COLS), D_MODEL),
        "b_gate": 0.05 * jax.random.normal(ks[3], (DEPTH, N_BRANCH * D_MODEL), f32),
        "lb_logits": 0.5 * jax.random.normal(ks[4], (DEPTH + 1, HG_WIDTH), f32),
        "hg_norm_g": 1.0 + 0.05 * jax.random.normal(ks[5], (DEPTH, HG_WIDTH), f32),
        "w_o_sb": nrm(ks[6], (DEPTH, SB_WIDTH, D_MODEL), SB_WIDTH),
        "w_o_hg": nrm(ks[7], (DEPTH, HG_WIDTH, D_MODEL), HG_WIDTH),
        "w_out": nrm(ks[8], (DEPTH, D_MODEL, D_MODEL), D_MODEL),
        "norm2_g": 1.0 + 0.05 * jax.random.normal(ks[9], (DEPTH, D_MODEL), f32),
        "w_ff1": nrm(ks[10], (DEPTH, D_MODEL, D_FF), D_MODEL),
        "w_ff2": nrm(ks[11], (DEPTH, D_FF, D_MODEL), D_FF),
        "final_g": 1.0 + 0.05 * jax.random.normal(ks[12], (D_MODEL,), f32),
    }


def reference(x, norm1_g, w_in, b_gate, lb_logits, hg_norm_g, w_o_sb, w_o_hg, w_out,
              norm2_g, w_ff1, w_ff2, final_g):
    h = x.astype(jnp.float32)
    for layer in range(DEPTH):
        h = hybrid_layer(h, layer, norm1_g, w_in, b_gate, lb_logits, hg_norm_g, w_o_sb, w_o_hg,
                         w_out, norm2_g, w_ff1, w_ff2)
    return rmsnorm(h, final_g).astype(x.dtype)
```

```python
import numpy as np
from contextlib import ExitStack
from functools import partial as P
import ml_dtypes
import concourse.bass as bass
import concourse.mybir as mybir
from concourse.bass_utils import run_bass_kernel_spmd

F32 = mybir.dt.float32
BF16 = mybir.dt.bfloat16
AF = mybir.ActivationFunctionType
ALU = mybir.AluOpType

D = 1024
S_ALL = 4096
S_OWN = 2048
NEG = -30000.0
EPS = 1e-6
import os
NOREUSEWAIT = bool(int(os.environ.get("NOREUSEWAIT", "0")))
REORDER = bool(int(os.environ.get("REORDER", "1")))
STRICT = bool(int(os.environ.get("STRICT", "1")))
ALPHA = float(os.environ.get("ALPHA", "0.05"))


class Tok:
    __slots__ = ("name", "writer", "readers")

    def __init__(self, name=""):
        self.name = name
        self.writer = None
        self.readers = {}


class Op:
    __slots__ = ("eng", "fn", "deps", "needs_inc", "cnt", "is_dma", "sem", "val", "_f", "snap", "cost", "idx", "fin")

    def __init__(self, eng, fn, is_dma):
        self.eng = eng
        self.fn = fn
        self.is_dma = is_dma
        self.deps = []
        self.needs_inc = False
        self.cnt = 0
        self.sem = None
        self.val = 0
        self._f = []
        self.snap = None
        self.cost = 0.5
        self.idx = 0
        self.fin = 0.0


class Sched:
    RING = {"sp": 16, "pool": 12}
    CENG = ("pe", "act", "dve", "pool")

    def __init__(self, nc):
        self.nc = nc
        self.h = {"pe": nc.tensor, "act": nc.scalar, "dve": nc.vector, "pool": nc.gpsimd, "sp": nc.sync}
        self.ops = []
        self.out_dmas = []

    def _deps(self, o, reads, writes):
        deps = {}
        for t in reads:
            if t.writer is not None:
                deps[id(t.writer)] = (t.writer, "raw")
        for t in writes:
            if t.writer is not None and id(t.writer) not in deps:
                deps[id(t.writer)] = (t.writer, "waw")
            for k, r in t.readers.items():
                rs = r if isinstance(r, list) else [r]
                for rr in rs:
                    if id(rr) not in deps:
                        deps[id(rr)] = (rr, "war")
        for t in reads:
            t.readers.setdefault(o.eng, [])
            lst = t.readers[o.eng]
            if len(lst) < 64:
                lst.append(o)
            else:
                if id(lst[-1]) not in deps:
                    deps[id(lst[-1])] = (lst[-1], "ord")
                lst[:] = [o]
        for t in writes:
            t.writer = o
            t.readers = {}
        deps.pop(id(o), None)
        o.deps = list(deps.values())

    @staticmethod
    def _nfree(fn):
        ap = None
        if hasattr(fn, "keywords") and "out" in fn.keywords:
            ap = fn.keywords["out"]
        elif hasattr(fn, "args") and fn.args:
            ap = fn.args[0]
        try:
            sh = ap.shape
            n = 1
            for d in sh[1:]:
                n *= int(d)
            return n, ap
        except Exception:
            return 512, None

    def op(self, eng, fn, reads=(), writes=(), cost=None):
        o = Op(eng, fn, False)
        self._deps(o, reads, writes)
        if cost is None:
            n, ap = self._nfree(fn)
            if eng == "pe":
                rhs = fn.args[2] if hasattr(fn, "args") and len(fn.args) > 2 else None
                try:
                    n = 1
                    for d in rhs.shape[1:]:
                        n *= int(d)
                except Exception:
                    pass
                cost = max(n, 64) / 2400.0 + 0.01
            elif eng == "act":
                cost = 0.12 + n / 1200.0
            elif eng == "dve":
                cost = 0.12 + n / 960.0
            else:
                cost = 0.2 + n / 420.0
        o.cost = cost
        o.idx = len(self.ops)
        self.ops.append(o)
        return o

    def dma(self, queue, fn, reads=(), writes=(), is_out=False):
        o = Op(queue, fn, True)
        self._deps(o, reads, writes)
        n, ap = self._nfree(fn)
        o.cost = 2.0 + n * 128 * 4 / 150e3
        o.idx = len(self.ops)
        self.ops.append(o)
        if is_out:
            self.out_dmas.append(o)
        return o

    def _reorder(self):
        import heapq
        LAT = 0.2
        segs = [[]]
        for o in self.ops:
            if o.eng == "bar":
                segs.append(o)
                segs.append([])
            else:
                segs[-1].append(o)
        new_ops = []
        tnow = 0.0
        for seg in segs:
            if not isinstance(seg, list):
                new_ops.append(seg)
                continue
            inseg = {id(o) for o in seg}
            indeg = {}
            succ = {}
            ready = {}
            for o in seg:
                k = 0
                for d, kind in o.deps:
                    if id(d) in inseg:
                        k += 1
                        succ.setdefault(id(d), []).append(o)
                indeg[id(o)] = k
                ready[id(o)] = tnow
            tail = {}
            for o in reversed(seg):
                t = 0.0
                for sc in succ.get(id(o), ()):
                    t = max(t, tail[id(sc)] + LAT)
                tail[id(o)] = t + min(o.cost, 3.0)
            heap = [(tnow - ALPHA * tail[id(o)], o.idx, o, tnow) for o in seg if indeg[id(o)] == 0]
            heapq.heapify(heap)
            free = {}
            tend = tnow
            while heap:
                _, _, o, r = heapq.heappop(heap)
                st_t = max(r, free.get(o.eng, tnow))
                if o.is_dma:
                    free[o.eng] = st_t + (0.8 if o.eng == "pool" else 0.15)
                    fin = st_t + o.cost
                else:
                    fin = st_t + o.cost
                    free[o.eng] = fin
                o.fin = fin
                tend = max(tend, fin)
                new_ops.append(o)
                for sc in succ.get(id(o), ()):
                    ready[id(sc)] = max(ready[id(sc)], fin + LAT)
                    indeg[id(sc)] -= 1
                    if indeg[id(sc)] == 0:
                        heapq.heappush(heap, (ready[id(sc)] - ALPHA * tail[id(sc)], sc.idx, sc, ready[id(sc)]))
            tnow = tend
        assert len(new_ops) == len(self.ops)
        self.ops = new_ops
        self.sim_us = tnow

    def barrier(self):
        o = Op("bar", None, False)
        o.idx = len(self.ops)
        self.ops.append(o)

    def _filtered(self, o):
        res = []
        for d, kind in o.deps:
            if d.is_dma or o.is_dma:
                res.append(d)
            elif d.eng == o.eng:
                if (kind == "raw" or STRICT) and o.eng != "pe":
                    res.append(d)
            else:
                res.append(d)
        return res

    def emit(self, ctx):
        nc = self.nc
        if REORDER:
            self._reorder()
        last = {}
        for pos, o in enumerate(self.ops):
            o.idx = pos
        for o in self.ops:
            if o.eng == "bar":
                for e, lo in last.items():
                    lo.needs_inc = True
                continue
            f = self._filtered(o)
            best = {}
            keep = []
            for d in f:
                if d.is_dma:
                    keep.append(d)
                elif d.eng not in best or best[d.eng].idx < d.idx:
                    best[d.eng] = d
            o._f = keep + list(best.values())
            for d in o._f:
                if not d.is_dma:
                    d.needs_inc = True
            if not o.is_dma:
                last[o.eng] = o
        esem = {e: ctx.enter_context(nc.semaphore("s_" + e)) for e in self.CENG}
        bsem = ctx.enter_context(nc.semaphore("s_bar"))
        rings = {q: [ctx.enter_context(nc.semaphore("r_%s%d" % (q, i))) for i in range(n)] for q, n in self.RING.items()}
        cnt = {e: 0 for e in esem}
        dcnt = {q: 0 for q in rings}
        rval = {}
        for o in self.ops:
            if o.eng == "bar":
                o.snap = (dict(cnt), dict(rval))
                continue
            if o.is_dma:
                k = dcnt[o.eng]
                dcnt[o.eng] = k + 1
                R = len(rings[o.eng])
                o.sem = rings[o.eng][k % R]
                o.val = 16 * (k // R + 1)
                rval[(o.eng, k % R)] = o.val
            else:
                if o.needs_inc:
                    cnt[o.eng] += 1
                o.cnt = cnt[o.eng]
                o.sem = esem[o.eng]
                o.val = o.cnt
        seen = {}
        nwait = 0
        nbar = 0
        for o in self.ops:
            if o.eng == "bar":
                c, rv = o.snap
                sp = self.h["sp"]
                for e, v in c.items():
                    if v > 0 and seen.get(("sp", id(esem[e])), 0) < v:
                        sp.wait_ge(esem[e], v)
                for (q, i), v in rv.items():
                    if seen.get(("sp", id(rings[q][i])), 0) < v:
                        sp.wait_ge(rings[q][i], v)
                nbar += 1
                sp.nop().then_inc(bsem, 1)
                for e in self.CENG:
                    self.h[e].wait_ge(bsem, nbar)
                for F in list(self.CENG) + ["sp"]:
                    for e, v in c.items():
                        seen[(F, id(esem[e]))] = max(seen.get((F, id(esem[e])), 0), v)
                    for (q, i), v in rv.items():
                        seen[(F, id(rings[q][i]))] = max(seen.get((F, id(rings[q][i])), 0), v)
                continue
            hdl = self.h[o.eng]
            waits = {}
            for d in o._f:
                key = id(d.sem)
                if key not in waits or waits[key][1] < d.val:
                    waits[key] = (d.sem, d.val)
            if o.is_dma and o.val > 16 and not NOREUSEWAIT:
                key = id(o.sem)
                v = o.val - 16
                if key not in waits or waits[key][1] < v:
                    waits[key] = (o.sem, v)
            selfwait = False
            for key, (sem, val) in waits.items():
                sk = (o.eng, key)
                if seen.get(sk, 0) >= val:
                    continue
                seen[sk] = val
                hdl.wait_ge(sem, val)
                nwait += 1
                if o.is_dma:
                    selfwait = True
            if selfwait:
                hdl.nop(nofuse=True)
            ins = o.fn()
            if o.is_dma:
                ins.then_inc(o.sem, 16)
            elif o.needs_inc:
                ins.then_inc(o.sem, 1)
        hdl = self.h["sp"]
        fin = {}
        for o in self.out_dmas:
            key = id(o.sem)
            if key not in fin or fin[key][1] < o.val:
                fin[key] = (o.sem, o.val)
        for key, (sem, val) in fin.items():
            if seen.get(("sp", key), 0) >= val:
                continue
            hdl.wait_ge(sem, val)
        self.stats = dict(nops=len(self.ops), nwait=nwait, cnt=cnt, dcnt=dcnt)


def build(dbg=0, upto="all"):
    nc = bass.Bass("TRN2", target_bir_lowering=False)

    def din(name, shape, dt=F32):
        return nc.dram_tensor(name, list(shape), dt, kind="ExternalInput").ap()

    x_all = din("x_all", [S_ALL, D])
    x_own = din("x_own", [S_OWN, D])
    w_in = din("w_in", [D, 5632])
    w_o_sb = din("w_o_sb", [512, D])
    w_o_hg = din("w_o_hg", [512, D])
    w_out = din("w_out", [D, D])
    w_ff1 = din("w_ff1", [D, 4096])
    w_ff2 = din("w_ff2", [4096, D])
    vecs_d = din("vecs", [128, 64])
    fgb_d = din("fgb", [128, D])
    cb_d = din("cbf", [128, 3072], BF16)
    scanm_d = din("scanm", [128, 512])
    y_own = nc.dram_tensor("y_own", [S_OWN, D], F32, kind="ExternalOutput").ap()
    dbg_out = nc.dram_tensor("dbg", [128, dbg], F32, kind="ExternalOutput").ap() if dbg else None

    ctx = ExitStack()
    with ctx:
        NB = 106400
        big = ctx.enter_context(nc.sbuf_tensor("big", [128, NB], BF16))
        pairs = [ctx.enter_context(nc.psum_tensor("pp%d" % i, [128, 1024], F32)) for i in range(4)]
        ps = [pairs[i // 2][:, (i % 2) * 512:(i % 2 + 1) * 512] for i in range(8)]
        ptok = [Tok("ps%d" % i) for i in range(8)]
        S = Sched(nc)
        A = nc.scalar.activation
        V = nc.vector
        st = {"off": 0}

        def alloc(shape, dt=BF16):
            n = int(np.prod(shape[1:]))
            nb = n * (2 if dt == F32 else 1)
            nbp = (nb + 15) // 16 * 16
            assert st["off"] + nbp <= st.get("lim", NB), ("sbuf overflow", st["off"], nbp, st.get("lim", NB))
            v = big[:, st["off"]:st["off"] + nb]
            st["off"] += nbp
            if dt == F32:
                v = v.bitcast(F32)
            if len(shape) == 3:
                v = v.rearrange("p (a b) -> p a b", a=shape[1])
            return v

        class Ring:
            def __init__(self, n, shape, dt=BF16):
                self.bufs = [(alloc(shape, dt), Tok()) for _ in range(n)]
                self.i = 0

            def next(self):
                b = self.bufs[self.i % len(self.bufs)]
                self.i += 1
                return b

        def mm(out, lhsT, rhs, start, stop, reads, writes):
            S.op("pe", P(nc.tensor.matmul, out, lhsT, rhs, start=start, stop=stop, skip_group_check=True), reads=reads, writes=writes)

        dstate = {"col": 0}

        def dump(ap, tok, n):
            if dbg_out is None:
                return
            c0 = dstate["col"]
            dstate["col"] += n
            assert dstate["col"] <= dbg
            tmpd = alloc([128, n], F32)
            td = Tok()
            S.op("dve", P(V.tensor_copy, out=tmpd, in_=ap), reads=[tok], writes=[td])
            S.dma("sp", P(nc.sync.dma_start, out=dbg_out[:, c0:c0 + n], in_=tmpd), reads=[td], is_out=True)

        vecs = alloc([128, 64], F32); tvec = Tok()
        cbf = alloc([128, 3072]); tcb = Tok()
        small = alloc([128, 32], F32); tsm = Tok()
        S.dma("sp", P(nc.sync.dma_start, out=vecs, in_=vecs_d), writes=[tvec])
        S.dma("sp", P(nc.sync.dma_start, out=cbf, in_=cb_d), writes=[tcb])
        identb = cbf[:, 0:128]; negtri = cbf[:, 128:256]; negones = cbf[:, 256:384]; onesb = cbf[:, 384:512]
        maskbd = cbf[:, 512:1024]; mhi = cbf[:, 1024:2048]; mlo = cbf[:, 2048:3072]
        g1t = vecs[:, 0:8]; g2t = vecs[:, 8:16]; nbg = vecs[:, 16:32]; lbl = vecs[:, 32:40]; gnt = vecs[:, 40:44]
        msel = vecs[:, 44:45]; omsel = vecs[:, 45:46]; pbg = vecs[:, 46:62]
        onec = small[:, 0:1]; epsc = small[:, 1:2]; lbv = small[:, 2:6]; lnoml = small[:, 6:10]; tmp4 = small[:, 10:14]
        S.op("pool", P(nc.gpsimd.memset, small, 0.0), writes=[tsm])
        S.op("pool", P(nc.gpsimd.memset, onec, 1.0), reads=[tsm], writes=[tsm])
        S.op("pool", P(nc.gpsimd.memset, epsc, EPS), reads=[tsm], writes=[tsm])
        mhalf = small[:, 14:15]
        S.op("pool", P(nc.gpsimd.memset, mhalf, -0.5), reads=[tsm], writes=[tsm])
        S.op("dve", P(V.tensor_tensor, out=tmp4, in0=lbl[:, 4:8], in1=lbl[:, 0:4], op=ALU.subtract), reads=[tvec, tsm], writes=[tsm])
        S.op("act", P(A, out=tmp4, in_=tmp4, func=AF.Exp), reads=[tsm], writes=[tsm])
        S.op("dve", P(V.tensor_scalar, out=tmp4, in0=tmp4, scalar1=1.0, scalar2=None, op0=ALU.add), reads=[tsm], writes=[tsm])
        S.op("dve", P(V.reciprocal, out=lbv, in_=tmp4), reads=[tsm], writes=[tsm])
        S.op("dve", P(V.tensor_scalar, out=tmp4, in0=lbv, scalar1=-1.0, scalar2=1.0, op0=ALU.mult, op1=ALU.add), reads=[tsm], writes=[tsm])
        S.op("act", P(A, out=lnoml, in_=tmp4, func=AF.Ln), reads=[tsm], writes=[tsm])

        o_hg_own = alloc([128, 4, S_OWN]); t_ohg = Tok()
        base0 = st["off"]

        def norm_T(R, src, src_is_dram, src_tok, gt, dst, dst_tok, xt_bank, pool_rstd=False):
            if src_is_dram:
                xb, tx = R["x"].next()
                S.dma("sp", P(nc.sync.dma_start, out=xb, in_=src), writes=[tx])
            else:
                xb, tx = src, src_tok
            xnb, tn = R["xnb"].next()
            sm, ts = R["stat"].next()
            S.op("pool", P(nc.gpsimd.memset, sm, 0.0), writes=[ts])
            S.op("act", P(A, out=xnb, in_=xb, func=AF.Square, accum_out=sm[:, 0:1]), reads=[tx, ts], writes=[tn, ts])
            if pool_rstd:
                S.op("dve", P(V.tensor_scalar, out=sm[:, 1:2], in0=sm[:, 0:1], scalar1=1.0 / D, scalar2=EPS, op0=ALU.mult, op1=ALU.add), reads=[ts], writes=[ts])
                S.op("pool", P(nc.gpsimd.tensor_tensor, out=sm[:, 2:3], in0=sm[:, 1:2], in1=mhalf, op=ALU.pow), reads=[ts, tsm], writes=[ts])
            else:
                S.op("act", P(A, out=sm[:, 1:2], in_=sm[:, 0:1], func=AF.Ln, scale=1.0 / D, bias=epsc), reads=[ts, tsm], writes=[ts])
                S.op("act", P(A, out=sm[:, 2:3], in_=sm[:, 1:2], func=AF.Exp, scale=-0.5), reads=[ts], writes=[ts])
            S.op("dve", P(V.tensor_scalar, out=xnb, in0=xb, scalar1=sm[:, 2:3], scalar2=None, op0=ALU.mult), reads=[tx, ts], writes=[tn])
            xtv = ps[xt_bank].bitcast(BF16).rearrange("p (a b) -> p a b", a=8)
            for c in range(8):
                S.op("pe", P(nc.tensor.transpose, xtv[:, c, :], xnb[:, c * 128:(c + 1) * 128], identb),
                     reads=[tn, tcb], writes=[ptok[xt_bank]])
            S.op("dve", P(V.tensor_tensor, out=dst, in0=xtv, in1=gt.unsqueeze(2).to_broadcast([128, 8, 128]), op=ALU.mult),
                 reads=[ptok[xt_bank], tvec], writes=[dst_tok])
            return xb, tx, sm, ts

        def norm_rings(nx=3, nxnb=2):
            return {"x": Ring(nx, [128, D], F32), "stat": Ring(3, [128, 4], F32), "xnb": Ring(nxnb, [128, D])}

        st["off"] = base0
        WB = alloc([128, 8, 2048]); tWB = [Tok() for _ in range(8)]
        for c in range(8):
            S.dma("pool", P(nc.gpsimd.dma_start, out=WB[:, c, :], in_=w_in[c * 128:(c + 1) * 128, 1536:3584]), writes=[tWB[c]])
        scanm = alloc([128, 512], F32); tscan = Tok()
        S.dma("sp", P(nc.sync.dma_start, out=scanm, in_=scanm_d), writes=[tscan])
        keep_off = st["off"]
        TOPA = NB - 12288
        st["off"] = TOPA
        WA = alloc([128, 8, 1536]); tWA = [Tok() for _ in range(8)]
        for c in range(8):
            S.dma("pool", P(nc.gpsimd.dma_start, out=WA[:, c, :], in_=w_in[c * 128:(c + 1) * 128, 0:1536]), writes=[tWA[c]])
        st["off"] = keep_off
        st["lim"] = TOPA
        RB = norm_rings(3, 2)
        xnTr = Ring(2, [128, 8, 512])
        Vh_r = Ring(2, [128, 4, 512])
        carry = alloc([128, 4, 128], F32); tcar = [Tok() for _ in range(4)]
        for h in range(4):
            S.op("pool", P(nc.gpsimd.memset, carry[:, h, :], 0.0), writes=[tcar[h]])
        FN = ["F", "Q", "Gg", "E", "Aa", "BC", "QE", "EB", "T3", "RS", "GE", "Osb"]
        BN = ["QsT", "KdT", "KlT", "Kltm", "ScT", "OSQ", "OH", "TB"]
        slots = []
        for s_i in range(2):
            sl = {}
            for nm in FN:
                sl[nm] = (alloc([128, 512], F32), Tok())
            for nm in BN:
                sl[nm] = (alloc([128, 512]), Tok())
            sl["Sall"] = (alloc([128, 8, 128], F32), Tok())
            sl["Sb"] = (alloc([128, 1024]), Tok())
            slots.append(sl)
        pring = {"i": 0}
        build.memB = st["off"]

        def S0(T):
            xnT, txn = xnTr.next()
            for sub in range(4):
                r0 = T * 512 + sub * 128
                norm_T(RB, x_all[r0:r0 + 128, :], True, None, g1t, xnT[:, :, sub * 128:(sub + 1) * 128], txn, 0)
            Vh, tVh = Vh_r.next()
            for sub in range(4):
                for c in range(8):
                    mm(ps[1], xnT[:, c, sub * 128:(sub + 1) * 128], WB[:, c, 512:1024], c == 0, c == 7, [txn, tWB[c]], [ptok[1]])
                S.op("act", P(A, out=Vh[:, sub, :], in_=ps[1], func=AF.Copy), reads=[ptok[1]], writes=[tVh])
            return dict(xnT=xnT, txn=txn, Vh=Vh, tVh=tVh)

        def head_stages(T, h, sl, tl):
            xnT, txn, Vh, tVh = tl["xnT"], tl["txn"], tl["Vh"], tl["tVh"]
            g = lambda nm: sl[nm]
            (F, tF), (Q, tQ), (Gg, tGg), (E, tE), (Aa, tA), (BC, tBC) = g("F"), g("Q"), g("Gg"), g("E"), g("Aa"), g("BC")
            (QE, tQE), (EB, tEB), (T3, tT3), (RS, tRS), (GE, tGE), (Osb, tOsb) = g("QE"), g("EB"), g("T3"), g("RS"), g("GE"), g("Osb")
            (QsT, tQs), (KdT, tKd), (KlT, tKl), (Kltm, tKt), (ScT, tSc), (OSQ, tOS), (OH, tOH), (TB, tTB) = [g(n) for n in BN]
            (Sall, tSa), (Sb, tSb) = g("Sall"), g("Sb")

            def st1():
                for (dst, tdst, col0) in ((F, tF, 0), (Q, tQ, 1024), (Gg, tGg, 1536)):
                    bank = 2 + (pring["i"] % 2)
                    pring["i"] += 1
                    for c in range(8):
                        mm(ps[bank], WB[:, c, col0 + h * 128: col0 + (h + 1) * 128], xnT[:, c, :], c == 0, c == 7, [txn, tWB[c]], [ptok[bank]])
                    if dst is F:
                        S.op("dve", P(V.tensor_copy, out=dst, in_=ps[bank]), reads=[ptok[bank]], writes=[tdst])
                    else:
                        S.op("act", P(A, out=dst, in_=ps[bank], func=AF.Copy), reads=[ptok[bank]], writes=[tdst])

            def st2():
                S.op("act", P(A, out=E, in_=F, func=AF.Exp, scale=-1.0), reads=[tF], writes=[tE])
                S.op("act", P(A, out=Aa, in_=E, func=AF.Ln, scale=lbv[:, h:h + 1], bias=onec), reads=[tE, tsm], writes=[tA])
                S.op("act", P(A, out=E, in_=E, func=AF.Ln, bias=onec), reads=[tE, tsm], writes=[tE])
                S.op("pool", P(nc.gpsimd.tensor_tensor, out=Aa, in0=Aa, in1=E, op=ALU.subtract), reads=[tA, tE], writes=[tA])
                S.op("dve", P(V.tensor_tensor_scan, out=BC, data0=scanm, data1=Aa, initial=0.0, op0=ALU.mult, op1=ALU.add),
                     reads=[tscan, tA], writes=[tBC])

            def st3():
                S.op("act", P(A, out=QE, in_=Q, func=AF.Exp, scale=-1.0), reads=[tQ], writes=[tQE])
                S.op("act", P(A, out=QE, in_=QE, func=AF.Ln, bias=onec), reads=[tQE, tsm], writes=[tQE])
                S.op("act", P(A, out=QE, in_=QE, func=AF.Exp, scale=-1.0), reads=[tQE], writes=[tQE])
                S.op("act", P(A, out=EB, in_=BC, func=AF.Exp), reads=[tBC], writes=[tEB])
                S.op("pool", P(nc.gpsimd.tensor_tensor, out=QE, in0=QE, in1=Q, op=ALU.mult), reads=[tQE, tQ], writes=[tQE])
                S.op("dve", P(V.tensor_tensor, out=QsT, in0=QE, in1=EB, op=ALU.mult), reads=[tQE, tEB], writes=[tQs])

            def st4():
                S.op("dve", P(V.tensor_tensor, out=E, in0=E, in1=BC, op=ALU.add), reads=[tE, tBC], writes=[tE])
                S.op("dve", P(V.scalar_tensor_tensor, out=F, in0=F, scalar=-1.0, in1=E, op0=ALU.mult, op1=ALU.subtract),
                     reads=[tF, tE], writes=[tF])
                S.op("act", P(A, out=KdT, in_=F, func=AF.Exp, bias=lnoml[:, h:h + 1]), reads=[tF, tsm], writes=[tKd])
                eb3 = EB.rearrange("p (c t) -> p c t", t=64)
                S.op("dve", P(V.tensor_tensor, out=KlT.rearrange("p (c t) -> p c t", t=64), in0=KdT.rearrange("p (c t) -> p c t", t=64),
                              in1=eb3[:, :, 63:64].to_broadcast([128, 8, 64]), op=ALU.mult), reads=[tKd, tEB], writes=[tKl])

            def st5():
                klv = ps[0].bitcast(BF16).rearrange("p (a b) -> p a b", a=8)
                for sub in range(4):
                    S.op("pe", P(nc.tensor.transpose, klv[:, sub, :], KlT[:, sub * 128:(sub + 1) * 128], identb), reads=[tKl, tcb], writes=[ptok[0]])
                S.op("act", P(A, out=Kltm.rearrange("p (a b) -> p a b", a=4), in_=klv[:, 0:4, :], func=AF.Copy), reads=[ptok[0]], writes=[tKt])

            def st6():
                for sub in range(4):
                    mm(ps[4][:, sub * 128:(sub + 1) * 128], KdT[:, sub * 128:(sub + 1) * 128], QsT[:, sub * 128:(sub + 1) * 128],
                       sub == 0, sub == 3, [tKd, tQs], [ptok[4]])
                S.op("dve", P(V.tensor_tensor, out=ScT, in0=ps[4], in1=maskbd, op=ALU.mult), reads=[ptok[4], tcb], writes=[tSc])

            def st7():
                for ch in range(8):
                    sub = ch // 2
                    pb = (ch % 2) * 64
                    bank = 6 + (ch % 2)
                    mm(ps[bank][:, sub * 128:(sub + 1) * 128], Kltm[pb:pb + 64, sub * 128:(sub + 1) * 128], Vh[pb:pb + 64, sub, h * 128:(h + 1) * 128],
                       True, True, [tKt, tVh], [ptok[bank]])
                S.op("dve", P(V.tensor_copy, out=Sall[:, 0, :], in_=carry[:, h, :]), reads=[tcar[h]], writes=[tSa])
                for ch in range(8):
                    sub = ch // 2
                    bank = 6 + (ch % 2)
                    c0 = ch * 64
                    if ch < 7:
                        S.op("dve", P(V.scalar_tensor_tensor, out=Sall[:, ch + 1, :], in0=Sall[:, ch, :], scalar=EB[:, c0 + 63:c0 + 64],
                                      in1=ps[bank][:, sub * 128:(sub + 1) * 128], op0=ALU.mult, op1=ALU.add),
                             reads=[tSa, tEB, ptok[bank]], writes=[tSa])
                    else:
                        S.op("dve", P(V.scalar_tensor_tensor, out=carry[:, h, :], in0=Sall[:, ch, :], scalar=EB[:, c0 + 63:c0 + 64],
                                      in1=ps[bank][:, sub * 128:(sub + 1) * 128], op0=ALU.mult, op1=ALU.add),
                             reads=[tSa, tEB, ptok[bank]], writes=[tcar[h]])
                S.op("pool", P(nc.gpsimd.tensor_copy, out=Sb, in_=Sall.rearrange("p a b -> p (a b)")), reads=[tSa], writes=[tSb])

            def st8():
                for sub in range(4):
                    mm(ps[5][:, sub * 128:(sub + 1) * 128], Vh[:, sub, h * 128:(h + 1) * 128], ScT[:, sub * 128:(sub + 1) * 128],
                       sub == 0, False, [tVh, tSc], [ptok[5]])
                for ch in range(8):
                    c0 = ch * 64
                    mm(ps[5][:, c0:c0 + 64], Sb[:, ch * 128:(ch + 1) * 128], QsT[:, c0:c0 + 64], False, ch == 7, [tSb, tQs], [ptok[5]])
                S.op("dve", P(V.tensor_copy, out=Osb, in_=ps[5]), reads=[ptok[5]], writes=[tOsb])

            def st9():
                S.op("act", P(A, out=OSQ, in_=Osb, func=AF.Square), reads=[tOsb], writes=[tOS])
                mm(ps[1], onesb, OSQ, True, True, [tcb, tOS], [ptok[1]])
                S.op("act", P(A, out=RS, in_=ps[1], func=AF.Ln, scale=1.0 / 128, bias=epsc), reads=[ptok[1], tsm], writes=[tRS])
                S.op("act", P(A, out=RS, in_=RS, func=AF.Exp, scale=-0.5), reads=[tRS], writes=[tRS])
                S.op("act", P(A, out=GE, in_=Gg, func=AF.Exp, scale=-1.0), reads=[tGg], writes=[tGE])
                S.op("act", P(A, out=GE, in_=GE, func=AF.Ln, bias=onec), reads=[tGE, tsm], writes=[tGE])
                S.op("act", P(A, out=GE, in_=GE, func=AF.Exp, scale=-1.0), reads=[tGE], writes=[tGE])
                S.op("pool", P(nc.gpsimd.tensor_tensor, out=GE, in0=GE, in1=Gg, op=ALU.mult), reads=[tGE, tGg], writes=[tGE])
                S.op("dve", P(V.tensor_tensor, out=RS, in0=RS, in1=Osb, op=ALU.mult), reads=[tRS, tOsb], writes=[tRS])
                S.op("dve", P(V.scalar_tensor_tensor, out=OH, in0=RS, scalar=gnt[:, h:h + 1], in1=GE, op0=ALU.mult, op1=ALU.mult),
                     reads=[tRS, tGE, tvec], writes=[tOH])
                oh4 = OH.rearrange("p (q two t) -> p q two t", two=2, t=128)
                tb3 = TB[:, 0:256].rearrange("p (q t) -> p q t", t=128)
                S.op("dve", P(V.tensor_scalar, out=tb3, in0=oh4[:, :, 1, :], scalar1=msel, scalar2=None, op0=ALU.mult), reads=[tOH, tvec], writes=[tTB])
                dst = o_hg_own[:, h, T * 256:(T + 1) * 256].rearrange("p (q t) -> p q t", t=128)
                S.op("dve", P(V.scalar_tensor_tensor, out=dst, in0=oh4[:, :, 0, :], scalar=omsel, in1=tb3, op0=ALU.mult, op1=ALU.add),
                     reads=[tOH, tTB, tvec], writes=[t_ohg])

            return [st1, st2, st3, st4, st5, st6, st7, st8, st9]

        NT_B = {"b1": 1, "b2": 2, "b3": 3}.get(upto, 8)
        NSL = len(slots)
        LAG = 9 // NSL + (1 if 9 % NSL else 0)
        tls = {}
        items = [(T, h) for T in range(NT_B) for h in range(4)]
        stage_lists = {}
        nsteps = (len(items) - 1) * LAG + 9
        tls[0] = S0(0)
        for step in range(nsteps):
            for k in range(len(items)):
                s_i = step - k * LAG
                if s_i < 0 or s_i >= 9:
                    continue
                T, h = items[k]
                if k not in stage_lists:
                    if h == 2 and T + 1 < NT_B and (T + 1) not in tls:
                        tls[T + 1] = S0(T + 1)
                    stage_lists[k] = head_stages(T, h, slots[k % NSL], tls[T])
                stage_lists[k][s_i]()
                if s_i == 8:
                    del stage_lists[k]
        if upto in ("b1", "b2", "b3", "b"):
            dump(o_hg_own[:, 0, 0:256], t_ohg, 256)
            dump(o_hg_own[:, 3, 0:256], t_ohg, 256)
            if upto == "b":
                dump(o_hg_own[:, 1, 1792:2048], t_ohg, 256)
            if upto == "b3":
                dump(o_hg_own[:, 1, 256:768], t_ohg, 512)
            S.emit(ctx)
            build.stats = S.stats
            return nc
        S.barrier()

        st["off"] = base0
        st["lim"] = TOPA
        o_sbT = alloc([128, 4, S_OWN]); t_osb = Tok()
        baseT1 = st["off"]
        KT = alloc([128, 4, S_ALL]); tKT = [Tok() for _ in range(32)]
        Vsb = alloc([128, 32, 512]); tV = [Tok() for _ in range(32)]
        QT = alloc([128, 4, S_OWN]); tQT = [Tok() for _ in range(16)]
        baseC = st["off"]
        RA = norm_rings(4, 3)
        xnTr = Ring(3, [128, 8, 512])

        def normA(idx):
            xnT, txn = xnTr.next()
            src = x_all if idx < 8 else x_own
            T = idx if idx < 8 else idx - 8
            for sub in range(4):
                r0 = T * 512 + sub * 128
                norm_T(RA, src[r0:r0 + 128, :], True, None, g1t, xnT[:, :, sub * 128:(sub + 1) * 128], txn, 0)
            return xnT, txn

        nxt = normA(0)
        for idx in range(12):
            xnT, txn = nxt
            if idx + 1 < 12:
                nxt = normA(idx + 1)
            if idx < 8:
                T = idx
                for hp in range(4):
                    bank = 1 + (hp % 2)
                    for c in range(8):
                        mm(ps[bank], WA[:, c, 512 + hp * 128: 512 + (hp + 1) * 128], xnT[:, c, :], c == 0, c == 7, [txn, tWA[c]], [ptok[bank]])
                    S.op("act", P(A, out=KT[:, hp, T * 512:(T + 1) * 512], in_=ps[bank], func=AF.Copy),
                         reads=[ptok[bank]], writes=[tKT[T * 4 + k] for k in range(4)])
                for sub in range(4):
                    bank = 3 + (sub % 2)
                    for c in range(8):
                        mm(ps[bank], xnT[:, c, sub * 128:(sub + 1) * 128], WA[:, c, 1024:1536], c == 0, c == 7, [txn, tWA[c]], [ptok[bank]])
                    S.op("dve", P(V.tensor_copy, out=Vsb[:, T * 4 + sub, :], in_=ps[bank]), reads=[ptok[bank]], writes=[tV[T * 4 + sub]])
            else:
                T = idx - 8
                for hp in range(4):
                    bank = 1 + (hp % 2)
                    for c in range(8):
                        mm(ps[bank], WA[:, c, hp * 128:(hp + 1) * 128], xnT[:, c, :], c == 0, c == 7, [txn, tWA[c]], [ptok[bank]])
                    S.op("act", P(A, out=QT[:, hp, T * 512:(T + 1) * 512], in_=ps[bank], func=AF.Copy, scale=0.125),
                         reads=[ptok[bank]], writes=[tQT[T * 4 + k] for k in range(4)])
        S.barrier()

        TOPC = NB - 24576
        st["off"] = TOPC
        st["lim"] = NB
        WG = alloc([128, 8, 2048]); tWG = [Tok() for _ in range(8)]
        WOS = alloc([128, 4, D]); tWOS = Tok()
        WOH = alloc([128, 4, D]); tWOH = Tok()
        for c in range(8):
            S.dma("pool", P(nc.gpsimd.dma_start, out=WG[:, c, :], in_=w_in[c * 128:(c + 1) * 128, 3584:5632]), writes=[tWG[c]])
        S.dma("pool", P(nc.gpsimd.dma_start, out=WOS, in_=w_o_sb.rearrange("(c p) n -> p c n", p=128)), writes=[tWOS])
        S.dma("pool", P(nc.gpsimd.dma_start, out=WOH, in_=w_o_hg.rearrange("(c p) n -> p c n", p=128)), writes=[tWOH])
        st["off"] = baseC
        st["lim"] = TOPC
        Er = Ring(3, [128, 1024], F32)
        SPr = Ring(3, [128, 1024])
        Sfr = Ring(2, [128, 1024], F32)
        Sbr = Ring(3, [128, 1024])
        Wr = Ring(3, [128, 1024])
        zb = [(0, 1), (2, 3), (4, 5)]
        units = []
        NSLOT = 16
        for j in range(NSLOT):
            nb = 2 * j + 2
            for i in range(nb):
                units.append(dict(j=j, i=i, kb=2 * j + 1 - i, last=(i == nb - 1)))
        prevS = {}

        def stage0(u, k):
            b0, b1 = zb[k % 3]
            u["zb"] = (b0, b1)
            j, kb = u["j"], u["kb"]
            for h in range(8):
                bank = b0 if h % 2 == 0 else b1
                pb = (h % 2) * 64
                mm(ps[bank][:, (h // 2) * 128:(h // 2 + 1) * 128], KT[pb:pb + 64, h // 2, kb * 128:(kb + 1) * 128],
                   QT[pb:pb + 64, h // 2, j * 128:(j + 1) * 128], h // 2 == 0, False, [tKT[kb], tQT[j]], [ptok[bank]])
            if u["i"] < 2:
                M = mhi if u["i"] == 0 else mlo
                for n, bank in enumerate((b0, b1)):
                    mm(ps[bank], identb, M[:, n * 512:(n + 1) * 512], False, False, [tcb], [ptok[bank]])

        def stage1(u, k):
            b0, b1 = u["zb"]
            E, tE = Er.next()
            SP, tSP = SPr.next()
            u["SP"] = (SP, tSP)
            zp = pairs[b0 // 2][:, :]
            S.op("act", P(A, out=E, in_=zp, func=AF.Exp), reads=[ptok[b0], ptok[b1]], writes=[tE])
            S.op("act", P(A, out=SP, in_=E, func=AF.Ln, bias=onec), reads=[tE, tsm], writes=[tSP])

        def stage2(u, k):
            b0, b1 = u["zb"]
            SP, tSP = u["SP"]
            i = u["i"]
            for n, bank in enumerate((b0, b1)):
                mm(ps[bank], negtri, SP[:, n * 512:(n + 1) * 512], False, i == 0, [tcb, tSP], [ptok[bank]])
            if i > 0:
                Sb, tSb = prevS["bf"]
                for n, bank in enumerate((b0, b1)):
                    mm(ps[bank], negones, Sb[:, n * 512:(n + 1) * 512], False, True, [tcb, tSb], [ptok[bank]])
            if not u["last"]:
                if i == 0:
                    prevS["bf"] = (SP, tSP)
                    prevS["f32"] = (SP, tSP)
                else:
                    Sp, tSp = prevS["f32"]
                    Sn, tSn = Sfr.next()
                    Sbn, tSbn = Sbr.next()
                    S.op("dve", P(V.tensor_tensor, out=Sbn, in0=Sp, in1=SP, op=ALU.add), reads=[tSp, tSP], writes=[tSbn])
                    S.op("dve", P(V.tensor_tensor, out=Sn, in0=Sp, in1=SP, op=ALU.add), reads=[tSp, tSP], writes=[tSn])
                    prevS["f32"] = (Sn, tSn)
                    prevS["bf"] = (Sbn, tSbn)

        def stage3(u, k):
            b0, b1 = u["zb"]
            W, tW = Wr.next()
            u["W"] = (W, tW)
            zp = pairs[b0 // 2][:, :]
            S.op("act", P(A, out=W, in_=zp, func=AF.Exp), reads=[ptok[b0], ptok[b1]], writes=[tW])

        def stage4(u, k):
            W, tW = u["W"]
            j, kb, i = u["j"], u["kb"], u["i"]
            OB = 6 + (j % 2)
            for h in range(8):
                pb = (h % 2) * 64
                mm(ps[OB][pb:pb + 64, (h // 2) * 128:(h // 2 + 1) * 128], Vsb[:, kb, h * 64:(h + 1) * 64], W[:, (h % 2) * 512 + (h // 2) * 128:(h % 2) * 512 + (h // 2 + 1) * 128],
                   (i == 0 and h < 2), u["last"] and h >= 6, [tV[kb], tW], [ptok[OB]])
            if u["last"]:
                S.op("dve", P(V.tensor_copy, out=o_sbT[:, :, j * 128:(j + 1) * 128], in_=ps[OB].rearrange("p (a b) -> p a b", a=4)),
                     reads=[ptok[OB]], writes=[t_osb])

        stages = [stage0, stage1, stage2, stage3, stage4]
        lag = [0, 1, 1, 2, 2]
        NU = len(units)
        for t in range(NU + 3):
            for s_i, fn in enumerate(stages):
                k = t - lag[s_i]
                if 0 <= k < NU:
                    fn(units[k], k)
        if upto == "c":
            dump(o_sbT[:, 0, 0:256], t_osb, 256)
            dump(o_sbT[:, 3, 1792:2048], t_osb, 256)
            S.emit(ctx)
            build.stats = S.stats
            return nc
        S.barrier()

        st["off"] = baseT1
        st["lim"] = TOPC
        h_all = alloc([128, 16, D], F32); th = [Tok() for _ in range(16)]
        baseT2 = st["off"]
        WOUT = alloc([128, 8, D]); tWO = [Tok() for _ in range(8)]
        for c in range(8):
            S.dma("pool", P(nc.gpsimd.dma_start, out=WOUT[:, c, :], in_=w_out[c * 128:(c + 1) * 128, :]), writes=[tWO[c]])
        RT = {"x": Ring(2, [128, D], F32), "stat": Ring(3, [128, 4], F32), "xnb": Ring(1, [128, D])}
        RXres = Ring(1, [128, D], F32)
        xnTr = Ring(1, [128, 8, 512])
        gr = Ring(4, [128, 512], F32)
        mTr = Ring(1, [128, 8, 512])
        for T in range(4):
            xnT, txn = xnTr.next()
            xkeep = []
            for sub in range(4):
                r0 = T * 512 + sub * 128
                xb, tx, _, _ = norm_T(RT, x_own[r0:r0 + 128, :], True, None, g1t, xnT[:, :, sub * 128:(sub + 1) * 128], txn, 0, pool_rstd=True)
                xkeep.append((xb, tx))
            mT, tmT = mTr.next()
            for dc in range(8):
                pbs, pbh = (3, 4) if dc % 2 == 0 else (5, 6)
                for br in range(2):
                    bank = 1 + br
                    for c in range(8):
                        mm(ps[bank], WG[:, c, br * 1024 + dc * 128: br * 1024 + (dc + 1) * 128], xnT[:, c, :], c == 0, c == 7, [txn, tWG[c]], [ptok[bank]])
                for hp in range(4):
                    mm(ps[pbs], WOS[:, hp, dc * 128:(dc + 1) * 128], o_sbT[:, hp, T * 512:(T + 1) * 512], hp == 0, hp == 3, [tWOS, t_osb], [ptok[pbs]])
                for hh in range(4):
                    mm(ps[pbh], WOH[:, hh, dc * 128:(dc + 1) * 128], o_hg_own[:, hh, T * 512:(T + 1) * 512], hh == 0, hh == 3, [tWOH, t_ohg], [ptok[pbh]])
                gs = []
                for br in range(2):
                    Gt, tGt = gr.next()
                    bcol = pbg[:, br * 8 + dc: br * 8 + dc + 1]
                    S.op("act", P(A, out=Gt, in_=ps[1 + br], func=AF.Sigmoid, bias=bcol), reads=[ptok[1 + br], tvec], writes=[tGt])
                    gs.append((Gt, tGt))
                (Ga, tGa), (Gb, tGb) = gs
                S.op("dve", P(V.tensor_tensor, out=Ga, in0=Ga, in1=ps[pbs], op=ALU.mult), reads=[tGa, ptok[pbs]], writes=[tGa])
                S.op("dve", P(V.tensor_tensor, out=Gb, in0=Gb, in1=ps[pbh], op=ALU.mult), reads=[tGb, ptok[pbh]], writes=[tGb])
                S.op("dve", P(V.tensor_tensor, out=mT[:, dc, :], in0=Ga, in1=Gb, op=ALU.add), reads=[tGa, tGb], writes=[tmT])
            for sub in range(4):
                blk = T * 4 + sub
                xb, tx = RXres.next()
                S.dma("sp", P(nc.sync.dma_start, out=xb, in_=x_own[blk * 128:(blk + 1) * 128, :]), writes=[tx])
                for n in range(2):
                    bank = (7, 3)[n]
                    for dc in range(8):
                        mm(ps[bank], mT[:, dc, sub * 128:(sub + 1) * 128], WOUT[:, dc, n * 512:(n + 1) * 512], dc == 0, dc == 7, [tmT, tWO[dc]], [ptok[bank]])
                    S.op("dve", P(V.tensor_tensor, out=h_all[:, blk, n * 512:(n + 1) * 512], in0=ps[bank],
                                                                                    in1=xb[:, n * 512:(n + 1) * 512], op=ALU.add),
                         reads=[ptok[bank], tx], writes=[th[blk]])
        S.barrier()

        st["off"] = baseT2
        st["lim"] = NB
        hnT = alloc([128, 8, S_OWN]); thn = [Tok() for _ in range(16)]
        fgb = alloc([128, D], F32); tfg = Tok()
        S.dma("sp", P(nc.sync.dma_start, out=fgb, in_=fgb_d), writes=[tfg])
        W1r = Ring(2, [128, 8, 512])
        W2r = Ring(2, [128, 4, 1024])
        RN = {"stat": Ring(3, [128, 4], F32), "xnb": Ring(2, [128, D])}
        aTr = Ring(2, [128, 4, 512])
        rlr = Ring(3, [128, 512], F32)
        yr = Ring(2, [128, D], F32)
        wbuf = []
        for k in range(2):
            W1, _ = W1r.next(); W2, _ = W2r.next()
            wbuf.append((W1, W2, [Tok() for _ in range(8)], [Tok() for _ in range(4)]))

        def load_q(q):
            W1, W2, t1s, t2s = wbuf[q % 2]
            for c in range(8):
                S.dma("pool", P(nc.gpsimd.dma_start, out=W1[:, c, :], in_=w_ff1[c * 128:(c + 1) * 128, q * 512:(q + 1) * 512]), writes=[t1s[c]])
            for c in range(4):
                r0 = q * 512 + c * 128
                S.dma("pool", P(nc.gpsimd.dma_start, out=W2[:, c, :], in_=w_ff2[r0:r0 + 128, :]), writes=[t2s[c]])
        load_q(0)
        load_q(1)
        for blk in range(16):
            norm_T(RN, h_all[:, blk, :], False, th[blk], g2t, hnT[:, :, blk * 128:(blk + 1) * 128], thn[blk], 0)
        NQ = 8
        for q in range(NQ):
            W1, W2, t1s, t2s = wbuf[q % 2]
            for T in range(4):
                aT, taT = aTr.next()
                for fc in range(4):
                    bank = 1 + (fc % 2)
                    for c in range(8):
                        mm(ps[bank], W1[:, c, fc * 128:(fc + 1) * 128], hnT[:, c, T * 512:(T + 1) * 512], c == 0, c == 7,
                           [t1s[c]] + thn[T * 4:(T + 1) * 4], [ptok[bank]])
                    RL, tRL = rlr.next()
                    S.op("act", P(A, out=RL, in_=ps[bank], func=AF.Relu), reads=[ptok[bank]], writes=[tRL])
                    S.op("dve", P(V.tensor_tensor, out=aT[:, fc, :], in0=RL, in1=RL, op=ALU.mult), reads=[tRL], writes=[taT])
                for sub in range(4):
                    blk = T * 4 + sub
                    for n in range(2):
                        bank = 3 + n + 2 * (sub % 2)
                        for fc in range(4):
                            mm(ps[bank], aT[:, fc, sub * 128:(sub + 1) * 128], W2[:, fc, n * 512:(n + 1) * 512], fc == 0, fc == 3, [taT, t2s[fc]], [ptok[bank]])
                        S.op("dve", P(V.tensor_tensor, out=h_all[:, blk, n * 512:(n + 1) * 512], in0=ps[bank],
                                                                                 in1=h_all[:, blk, n * 512:(n + 1) * 512], op=ALU.add),
                             reads=[ptok[bank], th[blk]], writes=[th[blk]])
                    if q == NQ - 1:
                        junk, tj = RN["xnb"].next()
                        sm, ts = RN["stat"].next()
                        S.op("pool", P(nc.gpsimd.memset, sm, 0.0), writes=[ts])
                        S.op("act", P(A, out=junk, in_=h_all[:, blk, :], func=AF.Square, accum_out=sm[:, 0:1]), reads=[th[blk], ts], writes=[tj, ts])
                        S.op("act", P(A, out=sm[:, 1:2], in_=sm[:, 0:1], func=AF.Ln, scale=1.0 / D, bias=epsc), reads=[ts, tsm], writes=[ts])
                        S.op("act", P(A, out=sm[:, 2:3], in_=sm[:, 1:2], func=AF.Exp, scale=-0.5), reads=[ts], writes=[ts])
                        Y, tY = yr.next()
                        S.op("dve", P(V.scalar_tensor_tensor, out=Y, in0=h_all[:, blk, :], scalar=sm[:, 2:3], in1=fgb, op0=ALU.mult, op1=ALU.mult),
                             reads=[th[blk], ts, tfg], writes=[tY])
                        S.dma("sp", P(nc.sync.dma_start, out=y_own[blk * 128:(blk + 1) * 128, :], in_=Y), reads=[tY], is_out=True)
            if q + 2 < NQ:
                load_q(q + 2)
        S.emit(ctx)
        build.stats = S.stats
    return nc


def host_inputs(inputs):
    bf = ml_dtypes.bfloat16
    x = np.asarray(inputs["x"], np.float32)
    f32 = lambda k: np.ascontiguousarray(np.asarray(inputs[k], np.float32))
    w_in = f32("w_in")[0]; w_o_sb = f32("w_o_sb")[0]; w_o_hg = f32("w_o_hg")[0]; w_out = f32("w_out")[0]
    w_ff1 = f32("w_ff1")[0]; w_ff2 = f32("w_ff2")[0]
    g1 = f32("norm1_g")[0]; g2 = f32("norm2_g")[0]; bg = f32("b_gate")[0]; lbl = f32("lb_logits"); gn = f32("hg_norm_g")[0]
    fg = f32("final_g")
    cb = np.zeros((128, 3072), np.float32)
    cb[:, 0:128] = np.eye(128)
    jj = np.arange(128)[:, None]; kk = np.arange(128)[None, :]
    cb[:, 128:256] = -((jj >= kk).astype(np.float32))
    cb[:, 256:384] = -1.0
    cb[:, 384:512] = 1.0
    bd = ((jj // 64) == (kk // 64)) & (jj <= kk)
    cb[:, 512:1024] = np.tile(bd.astype(np.float32), (1, 4))
    diag = np.where(jj < kk, 0.0, NEG)
    scanm = np.ones((128, 512), np.float32); scanm[:, ::64] = 0.0
    maps = []
    for c in range(8):
        b, a = c // 2, c % 2
        vec = np.zeros((128, 64), np.float32)
        vec[:, 0:8] = g1.reshape(8, 128).T
        vec[:, 8:16] = g2.reshape(8, 128).T
        vec[:, 16:32] = -bg.reshape(16, 128).T
        vec[:, 32:36] = lbl[0].reshape(4, 128).T
        vec[:, 36:40] = lbl[1].reshape(4, 128).T
        vec[:, 40:44] = gn.reshape(4, 128).T
        vec[:, 44] = float(a); vec[:, 45] = 1.0 - a
        vec[:, 46:62] = bg.reshape(16, 128).T
        cbc = cb.copy()
        if a == 1:
            cbc[:, 1024:2048] = np.tile(diag, (1, 8)); cbc[:, 2048:3072] = 0.0
        else:
            cbc[:, 1024:2048] = NEG; cbc[:, 2048:3072] = np.tile(diag, (1, 8))
        xb = x[b]
        xo = np.ascontiguousarray(xb.reshape(16, 2, 128, D)[:, a].reshape(S_OWN, D))
        maps.append({"x_all": np.ascontiguousarray(xb), "x_own": xo, "w_in": w_in, "w_o_sb": w_o_sb, "w_o_hg": w_o_hg, "w_out": w_out,
                     "w_ff1": w_ff1, "w_ff2": w_ff2, "vecs": vec, "fgb": np.ascontiguousarray(np.broadcast_to(fg[None, :], (128, D))),
                     "cbf": cbc.astype(bf), "scanm": scanm})
    return maps


def kernel(**inputs):
    maps = host_inputs(inputs)
    nc = build()
    res = run_bass_kernel_spmd(nc, maps, core_ids=list(range(8)))
    out = np.zeros((4, S_ALL, D), np.float32)
    for c in range(8):
        b, a = c // 2, c % 2
        y = np.asarray(res.results[c]["y_own"], np.float32).reshape(16, 128, D)
        out[b].reshape(16, 2, 128, D)[:, a] = y
    return out
```

```python
import numpy as np
from contextlib import ExitStack
from functools import partial as P
import ml_dtypes
import concourse.bass as bass
import concourse.mybir as mybir
from concourse.bass_utils import run_bass_kernel_spmd

F32 = mybir.dt.float32
BF16 = mybir.dt.bfloat16
AF = mybir.ActivationFunctionType
ALU = mybir.AluOpType

D = 1024
S_ALL = 4096
S_OWN = 2048
NEG = -30000.0
EPS = 1e-6
import os
NOREUSEWAIT = bool(int(os.environ.get("NOREUSEWAIT", "0")))
REORDER = bool(int(os.environ.get("REORDER", "1")))
STRICT = bool(int(os.environ.get("STRICT", "1")))
ALPHA = float(os.environ.get("ALPHA", "0.05"))


class Tok:
    __slots__ = ("name", "writer", "readers")

    def __init__(self, name=""):
        self.name = name
        self.writer = None
        self.readers = {}


class Op:
    __slots__ = ("eng", "fn", "deps", "needs_inc", "cnt", "is_dma", "sem", "val", "_f", "snap", "cost", "idx", "fin")

    def __init__(self, eng, fn, is_dma):
        self.eng = eng
        self.fn = fn
        self.is_dma = is_dma
        self.deps = []
        self.needs_inc = False
        self.cnt = 0
        self.sem = None
        self.val = 0
        self._f = []
        self.snap = None
        self.cost = 0.5
        self.idx = 0
        self.fin = 0.0


class Sched:
    RING = {"sp": 16, "pool": 12}
    CENG = ("pe", "act", "dve", "pool")

    def __init__(self, nc):
        self.nc = nc
        self.h = {"pe": nc.tensor, "act": nc.scalar, "dve": nc.vector, "pool": nc.gpsimd, "sp": nc.sync}
        self.ops = []
        self.out_dmas = []

    def _deps(self, o, reads, writes):
        deps = {}
        for t in reads:
            if t.writer is not None:
                deps[id(t.writer)] = (t.writer, "raw")
        for t in writes:
            if t.writer is not None and id(t.writer) not in deps:
                deps[id(t.writer)] = (t.writer, "waw")
            for k, r in t.readers.items():
                rs = r if isinstance(r, list) else [r]
                for rr in rs:
                    if id(rr) not in deps:
                        deps[id(rr)] = (rr, "war")
        for t in reads:
            t.readers.setdefault(o.eng, [])
            lst = t.readers[o.eng]
            if len(lst) < 64:
                lst.append(o)
            else:
                if id(lst[-1]) not in deps:
                    deps[id(lst[-1])] = (lst[-1], "ord")
                lst[:] = [o]
        for t in writes:
            t.writer = o
            t.readers = {}
        deps.pop(id(o), None)
        o.deps = list(deps.values())

    @staticmethod
    def _nfree(fn):
        ap = None
        if hasattr(fn, "keywords") and "out" in fn.keywords:
            ap = fn.keywords["out"]
        elif hasattr(fn, "args") and fn.args:
            ap = fn.args[0]
        try:
            sh = ap.shape
            n = 1
            for d in sh[1:]:
                n *= int(d)
            return n, ap
        except Exception:
            return 512, None

    def op(self, eng, fn, reads=(), writes=(), cost=None):
        o = Op(eng, fn, False)
        self._deps(o, reads, writes)
        if cost is None:
            n, ap = self._nfree(fn)
            if eng == "pe":
                rhs = fn.args[2] if hasattr(fn, "args") and len(fn.args) > 2 else None
                try:
                    n = 1
                    for d in rhs.shape[1:]:
                        n *= int(d)
                except Exception:
                    pass
                cost = max(n, 64) / 2400.0 + 0.01
            elif eng == "act":
                cost = 0.12 + n / 1200.0
            elif eng == "dve":
                cost = 0.12 + n / 960.0
            else:
                cost = 0.2 + n / 420.0
        o.cost = cost
        o.idx = len(self.ops)
        self.ops.append(o)
        return o

    def dma(self, queue, fn, reads=(), writes=(), is_out=False):
        o = Op(queue, fn, True)
        self._deps(o, reads, writes)
        n, ap = self._nfree(fn)
        o.cost = 2.0 + n * 128 * 4 / 150e3
        o.idx = len(self.ops)
        self.ops.append(o)
        if is_out:
            self.out_dmas.append(o)
        return o

    def _reorder(self):
        import heapq
        LAT = 0.2
        segs = [[]]
        for o in self.ops:
            if o.eng == "bar":
                segs.append(o)
                segs.append([])
            else:
                segs[-1].append(o)
        new_ops = []
        tnow = 0.0
        for seg in segs:
            if not isinstance(seg, list):
                new_ops.append(seg)
                continue
            inseg = {id(o) for o in seg}
            indeg = {}
            succ = {}
            ready = {}
            for o in seg:
                k = 0
                for d, kind in o.deps:
                    if id(d) in inseg:
                        k += 1
                        succ.setdefault(id(d), []).append(o)
                indeg[id(o)] = k
                ready[id(o)] = tnow
            tail = {}
            for o in reversed(seg):
                t = 0.0
                for sc in succ.get(id(o), ()):
                    t = max(t, tail[id(sc)] + LAT)
                tail[id(o)] = t + min(o.cost, 3.0)
            heap = [(tnow - ALPHA * tail[id(o)], o.idx, o, tnow) for o in seg if indeg[id(o)] == 0]
            heapq.heapify(heap)
            free = {}
            tend = tnow
            while heap:
                _, _, o, r = heapq.heappop(heap)
                st_t = max(r, free.get(o.eng, tnow))
                if o.is_dma:
                    free[o.eng] = st_t + (0.8 if o.eng == "pool" else 0.15)
                    fin = st_t + o.cost
                else:
                    fin = st_t + o.cost
                    free[o.eng] = fin
                o.fin = fin
                tend = max(tend, fin)
                new_ops.append(o)
                for sc in succ.get(id(o), ()):
                    ready[id(sc)] = max(ready[id(sc)], fin + LAT)
                    indeg[id(sc)] -= 1
                    if indeg[id(sc)] == 0:
                        heapq.heappush(heap, (ready[id(sc)] - ALPHA * tail[id(sc)], sc.idx, sc, ready[id(sc)]))
            tnow = tend
        assert len(new_ops) == len(self.ops)
        self.ops = new_ops
        self.sim_us = tnow

    def barrier(self):
        o = Op("bar", None, False)
        o.idx = len(self.ops)
        self.ops.append(o)

    def _filtered(self, o):
        res = []
        for d, kind in o.deps:
            if d.is_dma or o.is_dma:
                res.append(d)
            elif d.eng == o.eng:
                if (kind == "raw" or STRICT) and o.eng != "pe":
                    res.append(d)
            else:
                res.append(d)
        return res

    def emit(self, ctx):
        nc = self.nc
        if REORDER:
            self._reorder()
        last = {}
        for pos, o in enumerate(self.ops):
            o.idx = pos
        for o in self.ops:
            if o.eng == "bar":
                for e, lo in last.items():
                    lo.needs_inc = True
                continue
            f = self._filtered(o)
            best = {}
            keep = []
            for d in f:
                if d.is_dma:
                    keep.append(d)
                elif d.eng not in best or best[d.eng].idx < d.idx:
                    best[d.eng] = d
            o._f = keep + list(best.values())
            for d in o._f:
                if not d.is_dma:
                    d.needs_inc = True
            if not o.is_dma:
                last[o.eng] = o
        esem = {e: ctx.enter_context(nc.semaphore("s_" + e)) for e in self.CENG}
        bsem = ctx.enter_context(nc.semaphore("s_bar"))
        rings = {q: [ctx.enter_context(nc.semaphore("r_%s%d" % (q, i))) for i in range(n)] for q, n in self.RING.items()}
        cnt = {e: 0 for e in esem}
        dcnt = {q: 0 for q in rings}
        rval = {}
        for o in self.ops:
            if o.eng == "bar":
                o.snap = (dict(cnt), dict(rval))
                continue
            if o.is_dma:
                k = dcnt[o.eng]
                dcnt[o.eng] = k + 1
                R = len(rings[o.eng])
                o.sem = rings[o.eng][k % R]
                o.val = 16 * (k // R + 1)
                rval[(o.eng, k % R)] = o.val
            else:
                if o.needs_inc:
                    cnt[o.eng] += 1
                o.cnt = cnt[o.eng]
                o.sem = esem[o.eng]
                o.val = o.cnt
        seen = {}
        nwait = 0
        nbar = 0
        for o in self.ops:
            if o.eng == "bar":
                c, rv = o.snap
                sp = self.h["sp"]
                for e, v in c.items():
                    if v > 0 and seen.get(("sp", id(esem[e])), 0) < v:
                        sp.wait_ge(esem[e], v)
                for (q, i), v in rv.items():
                    if seen.get(("sp", id(rings[q][i])), 0) < v:
                        sp.wait_ge(rings[q][i], v)
                nbar += 1
                sp.nop().then_inc(bsem, 1)
                for e in self.CENG:
                    self.h[e].wait_ge(bsem, nbar)
                for F in list(self.CENG) + ["sp"]:
                    for e, v in c.items():
                        seen[(F, id(esem[e]))] = max(seen.get((F, id(esem[e])), 0), v)
                    for (q, i), v in rv.items():
                        seen[(F, id(rings[q][i]))] = max(seen.get((F, id(rings[q][i])), 0), v)
                continue
            hdl = self.h[o.eng]
            waits = {}
            for d in o._f:
                key = id(d.sem)
                if key not in waits or waits[key][1] < d.val:
                    waits[key] = (d.sem, d.val)
            if o.is_dma and o.val > 16 and not NOREUSEWAIT:
                key = id(o.sem)
                v = o.val - 16
                if key not in waits or waits[key][1] < v:
                    waits[key] = (o.sem, v)
            selfwait = False
            for key, (sem, val) in waits.items():
                sk = (o.eng, key)
                if seen.get(sk, 0) >= val:
                    continue
                seen[sk] = val
                hdl.wait_ge(sem, val)
                nwait += 1
                if o.is_dma:
                    selfwait = True
            if selfwait:
                hdl.nop(nofuse=True)
            ins = o.fn()
            if o.is_dma:
                ins.then_inc(o.sem, 16)
            elif o.needs_inc:
                ins.then_inc(o.sem, 1)
        hdl = self.h["sp"]
        fin = {}
        for o in self.out_dmas:
            key = id(o.sem)
            if key not in fin or fin[key][1] < o.val:
                fin[key] = (o.sem, o.val)
        for key, (sem, val) in fin.items():
            if seen.get(("sp", key), 0) >= val:
                continue
            hdl.wait_ge(sem, val)
        self.stats = dict(nops=len(self.ops), nwait=nwait, cnt=cnt, dcnt=dcnt)


def build(dbg=0, upto="all"):
    nc = bass.Bass("TRN2", target_bir_lowering=False)

    def din(name, shape, dt=F32):
        return nc.dram_tensor(name, list(shape), dt, kind="ExternalInput").ap()

    x_all = din("x_all", [S_ALL, D])
    x_own = din("x_own", [S_OWN, D])
    w_in = din("w_in", [D, 5632])
    w_o_sb = din("w_o_sb", [512, D])
    w_o_hg = din("w_o_hg", [512, D])
    w_out = din("w_out", [D, D])
    w_ff1 = din("w_ff1", [D, 4096])
    w_ff2 = din("w_ff2", [4096, D])
    vecs_d = din("vecs", [128, 64])
    fgb_d = din("fgb", [128, D])
    cb_d = din("cbf", [128, 3072], BF16)
    scanm_d = din("scanm", [128, 512])
    y_own = nc.dram_tensor("y_own", [S_OWN, D], F32, kind="ExternalOutput").ap()
    dbg_out = nc.dram_tensor("dbg", [128, dbg], F32, kind="ExternalOutput").ap() if dbg else None

    ctx = ExitStack()
    with ctx:
        NB = 106400
        big = ctx.enter_context(nc.sbuf_tensor("big", [128, NB], BF16))
        pairs = [ctx.enter_context(nc.psum_tensor("pp%d" % i, [128, 1024], F32)) for i in range(4)]
        ps = [pairs[i // 2][:, (i % 2) * 512:(i % 2 + 1) * 512] for i in range(8)]
        ptok = [Tok("ps%d" % i) for i in range(8)]
        S = Sched(nc)
        A = nc.scalar.activation
        V = nc.vector
        st = {"off": 0}

        def alloc(shape, dt=BF16):
            n = int(np.prod(shape[1:]))
            nb = n * (2 if dt == F32 else 1)
            nbp = (nb + 15) // 16 * 16
            assert st["off"] + nbp <= st.get("lim", NB), ("sbuf overflow", st["off"], nbp, st.get("lim", NB))
            v = big[:, st["off"]:st["off"] + nb]
            st["off"] += nbp
            if dt == F32:
                v = v.bitcast(F32)
            if len(shape) == 3:
                v = v.rearrange("p (a b) -> p a b", a=shape[1])
            return v

        class Ring:
            def __init__(self, n, shape, dt=BF16):
                self.bufs = [(alloc(shape, dt), Tok()) for _ in range(n)]
                self.i = 0

            def next(self):
                b = self.bufs[self.i % len(self.bufs)]
                self.i += 1
                return b

        def mm(out, lhsT, rhs, start, stop, reads, writes):
            S.op("pe", P(nc.tensor.matmul, out, lhsT, rhs, start=start, stop=stop, skip_group_check=True), reads=reads, writes=writes)

        dstate = {"col": 0}

        def dump(ap, tok, n):
            if dbg_out is None:
                return
            c0 = dstate["col"]
            dstate["col"] += n
            assert dstate["col"] <= dbg
            tmpd = alloc([128, n], F32)
            td = Tok()
            S.op("dve", P(V.tensor_copy, out=tmpd, in_=ap), reads=[tok], writes=[td])
            S.dma("sp", P(nc.sync.dma_start, out=dbg_out[:, c0:c0 + n], in_=tmpd), reads=[td], is_out=True)

        vecs = alloc([128, 64], F32); tvec = Tok()
        cbf = alloc([128, 3072]); tcb = Tok()
        small = alloc([128, 32], F32); tsm = Tok()
        S.dma("sp", P(nc.sync.dma_start, out=vecs, in_=vecs_d), writes=[tvec])
        S.dma("sp", P(nc.sync.dma_start, out=cbf, in_=cb_d), writes=[tcb])
        identb = cbf[:, 0:128]; negtri = cbf[:, 128:256]; negones = cbf[:, 256:384]; onesb = cbf[:, 384:512]
        maskbd = cbf[:, 512:1024]; mhi = cbf[:, 1024:2048]; mlo = cbf[:, 2048:3072]
        g1t = vecs[:, 0:8]; g2t = vecs[:, 8:16]; nbg = vecs[:, 16:32]; lbl = vecs[:, 32:40]; gnt = vecs[:, 40:44]
        msel = vecs[:, 44:45]; omsel = vecs[:, 45:46]; pbg = vecs[:, 46:62]
        onec = small[:, 0:1]; epsc = small[:, 1:2]; lbv = small[:, 2:6]; lnoml = small[:, 6:10]; tmp4 = small[:, 10:14]
        S.op("pool", P(nc.gpsimd.memset, small, 0.0), writes=[tsm])
        S.op("pool", P(nc.gpsimd.memset, onec, 1.0), reads=[tsm], writes=[tsm])
        S.op("pool", P(nc.gpsimd.memset, epsc, EPS), reads=[tsm], writes=[tsm])
        mhalf = small[:, 14:15]
        S.op("pool", P(nc.gpsimd.memset, mhalf, -0.5), reads=[tsm], writes=[tsm])
        S.op("dve", P(V.tensor_tensor, out=tmp4, in0=lbl[:, 4:8], in1=lbl[:, 0:4], op=ALU.subtract), reads=[tvec, tsm], writes=[tsm])
        S.op("act", P(A, out=tmp4, in_=tmp4, func=AF.Exp), reads=[tsm], writes=[tsm])
        S.op("dve", P(V.tensor_scalar, out=tmp4, in0=tmp4, scalar1=1.0, scalar2=None, op0=ALU.add), reads=[tsm], writes=[tsm])
        S.op("dve", P(V.reciprocal, out=lbv, in_=tmp4), reads=[tsm], writes=[tsm])
        S.op("dve", P(V.tensor_scalar, out=tmp4, in0=lbv, scalar1=-1.0, scalar2=1.0, op0=ALU.mult, op1=ALU.add), reads=[tsm], writes=[tsm])
        S.op("act", P(A, out=lnoml, in_=tmp4, func=AF.Ln), reads=[tsm], writes=[tsm])

        o_hg_own = alloc([128, 4, S_OWN]); t_ohg = Tok()
        base0 = st["off"]

        def norm_T(R, src, src_is_dram, src_tok, gt, dst, dst_tok, xt_bank, pool_rstd=False):
            if src_is_dram:
                xb, tx = R["x"].next()
                S.dma("sp", P(nc.sync.dma_start, out=xb, in_=src), writes=[tx])
            else:
                xb, tx = src, src_tok
            xnb, tn = R["xnb"].next()
            sm, ts = R["stat"].next()
            S.op("pool", P(nc.gpsimd.memset, sm, 0.0), writes=[ts])
            S.op("act", P(A, out=xnb, in_=xb, func=AF.Square, accum_out=sm[:, 0:1]), reads=[tx, ts], writes=[tn, ts])
            if pool_rstd:
                S.op("dve", P(V.tensor_scalar, out=sm[:, 1:2], in0=sm[:, 0:1], scalar1=1.0 / D, scalar2=EPS, op0=ALU.mult, op1=ALU.add), reads=[ts], writes=[ts])
                S.op("pool", P(nc.gpsimd.tensor_tensor, out=sm[:, 2:3], in0=sm[:, 1:2], in1=mhalf, op=ALU.pow), reads=[ts, tsm], writes=[ts])
            else:
                S.op("act", P(A, out=sm[:, 1:2], in_=sm[:, 0:1], func=AF.Ln, scale=1.0 / D, bias=epsc), reads=[ts, tsm], writes=[ts])
                S.op("act", P(A, out=sm[:, 2:3], in_=sm[:, 1:2], func=AF.Exp, scale=-0.5), reads=[ts], writes=[ts])
            S.op("dve", P(V.tensor_scalar, out=xnb, in0=xb, scalar1=sm[:, 2:3], scalar2=None, op0=ALU.mult), reads=[tx, ts], writes=[tn])
            xtv = ps[xt_bank].bitcast(BF16).rearrange("p (a b) -> p a b", a=8)
            for c in range(8):
                S.op("pe", P(nc.tensor.transpose, xtv[:, c, :], xnb[:, c * 128:(c + 1) * 128], identb),
                     reads=[tn, tcb], writes=[ptok[xt_bank]])
            S.op("dve", P(V.tensor_tensor, out=dst, in0=xtv, in1=gt.unsqueeze(2).to_broadcast([128, 8, 128]), op=ALU.mult),
                 reads=[ptok[xt_bank], tvec], writes=[dst_tok])
            return xb, tx, sm, ts

        def norm_rings(nx=3, nxnb=2):
            return {"x": Ring(nx, [128, D], F32), "stat": Ring(3, [128, 4], F32), "xnb": Ring(nxnb, [128, D])}

        st["off"] = base0
        WB = alloc([128, 8, 2048]); tWB = [Tok() for _ in range(8)]
        for c in range(8):
            S.dma("pool", P(nc.gpsimd.dma_start, out=WB[:, c, :], in_=w_in[c * 128:(c + 1) * 128, 1536:3584]), writes=[tWB[c]])
        scanm = alloc([128, 512], F32); tscan = Tok()
        S.dma("sp", P(nc.sync.dma_start, out=scanm, in_=scanm_d), writes=[tscan])
        keep_off = st["off"]
        TOPA = NB - 12288
        st["off"] = TOPA
        WA = alloc([128, 8, 1536]); tWA = [Tok() for _ in range(8)]
        st["off"] = keep_off
        st["lim"] = TOPA
        RB = norm_rings(3, 2)
        xnTr = Ring(2, [128, 8, 512])
        Vh_r = Ring(2, [128, 4, 512])
        carry = alloc([128, 4, 128], F32); tcar = [Tok() for _ in range(4)]
        for h in range(4):
            S.op("pool", P(nc.gpsimd.memset, carry[:, h, :], 0.0), writes=[tcar[h]])
        FN = ["F", "Q", "Gg", "E", "Aa", "BC", "QE", "EB", "T3", "RS", "GE", "Osb"]
        BN = ["QsT", "KdT", "KlT", "Kltm", "ScT", "OSQ", "OH", "TB"]
        slots = []
        for s_i in range(2):
            sl = {}
            for nm in FN:
                sl[nm] = (alloc([128, 512], F32), Tok())
            for nm in BN:
                sl[nm] = (alloc([128, 512]), Tok())
            sl["Sall"] = (alloc([128, 8, 128], F32), Tok())
            sl["Sb"] = (alloc([128, 1024]), Tok())
            slots.append(sl)
        pring = {"i": 0}
        build.memB = st["off"]

        def S0(T):
            xnT, txn = xnTr.next()
            for sub in range(4):
                r0 = T * 512 + sub * 128
                norm_T(RB, x_all[r0:r0 + 128, :], True, None, g1t, xnT[:, :, sub * 128:(sub + 1) * 128], txn, 0)
            Vh, tVh = Vh_r.next()
            for sub in range(4):
                for c in range(8):
                    mm(ps[1], xnT[:, c, sub * 128:(sub + 1) * 128], WB[:, c, 512:1024], c == 0, c == 7, [txn, tWB[c]], [ptok[1]])
                S.op("act", P(A, out=Vh[:, sub, :], in_=ps[1], func=AF.Copy), reads=[ptok[1]], writes=[tVh])
            return dict(xnT=xnT, txn=txn, Vh=Vh, tVh=tVh)

        def head_stages(T, h, sl, tl):
            xnT, txn, Vh, tVh = tl["xnT"], tl["txn"], tl["Vh"], tl["tVh"]
            g = lambda nm: sl[nm]
            (F, tF), (Q, tQ), (Gg, tGg), (E, tE), (Aa, tA), (BC, tBC) = g("F"), g("Q"), g("Gg"), g("E"), g("Aa"), g("BC")
            (QE, tQE), (EB, tEB), (T3, tT3), (RS, tRS), (GE, tGE), (Osb, tOsb) = g("QE"), g("EB"), g("T3"), g("RS"), g("GE"), g("Osb")
            (QsT, tQs), (KdT, tKd), (KlT, tKl), (Kltm, tKt), (ScT, tSc), (OSQ, tOS), (OH, tOH), (TB, tTB) = [g(n) for n in BN]
            (Sall, tSa), (Sb, tSb) = g("Sall"), g("Sb")

            def st1():
                for (dst, tdst, col0) in ((F, tF, 0), (Q, tQ, 1024), (Gg, tGg, 1536)):
                    bank = 2 + (pring["i"] % 2)
                    pring["i"] += 1
                    for c in range(8):
                        mm(ps[bank], WB[:, c, col0 + h * 128: col0 + (h + 1) * 128], xnT[:, c, :], c == 0, c == 7, [txn, tWB[c]], [ptok[bank]])
                    if dst is F:
                        S.op("dve", P(V.tensor_copy, out=dst, in_=ps[bank]), reads=[ptok[bank]], writes=[tdst])
                    else:
                        S.op("act", P(A, out=dst, in_=ps[bank], func=AF.Copy), reads=[ptok[bank]], writes=[tdst])

            def st2():
                S.op("act", P(A, out=E, in_=F, func=AF.Exp, scale=-1.0), reads=[tF], writes=[tE])
                S.op("act", P(A, out=Aa, in_=E, func=AF.Ln, scale=lbv[:, h:h + 1], bias=onec), reads=[tE, tsm], writes=[tA])
                S.op("act", P(A, out=E, in_=E, func=AF.Ln, bias=onec), reads=[tE, tsm], writes=[tE])
                S.op("pool", P(nc.gpsimd.tensor_tensor, out=Aa, in0=Aa, in1=E, op=ALU.subtract), reads=[tA, tE], writes=[tA])
                S.op("dve", P(V.tensor_tensor_scan, out=BC, data0=scanm, data1=Aa, initial=0.0, op0=ALU.mult, op1=ALU.add),
                     reads=[tscan, tA], writes=[tBC])

            def st3():
                S.op("act", P(A, out=QE, in_=Q, func=AF.Exp, scale=-1.0), reads=[tQ], writes=[tQE])
                S.op("act", P(A, out=QE, in_=QE, func=AF.Ln, bias=onec), reads=[tQE, tsm], writes=[tQE])
                S.op("act", P(A, out=QE, in_=QE, func=AF.Exp, scale=-1.0), reads=[tQE], writes=[tQE])
                S.op("act", P(A, out=EB, in_=BC, func=AF.Exp), reads=[tBC], writes=[tEB])
                S.op("pool", P(nc.gpsimd.tensor_tensor, out=QE, in0=QE, in1=Q, op=ALU.mult), reads=[tQE, tQ], writes=[tQE])
                S.op("dve", P(V.tensor_tensor, out=QsT, in0=QE, in1=EB, op=ALU.mult), reads=[tQE, tEB], writes=[tQs])

            def st4():
                S.op("dve", P(V.tensor_tensor, out=E, in0=E, in1=BC, op=ALU.add), reads=[tE, tBC], writes=[tE])
                S.op("dve", P(V.scalar_tensor_tensor, out=F, in0=F, scalar=-1.0, in1=E, op0=ALU.mult, op1=ALU.subtract),
                     reads=[tF, tE], writes=[tF])
                S.op("act", P(A, out=KdT, in_=F, func=AF.Exp, bias=lnoml[:, h:h + 1]), reads=[tF, tsm], writes=[tKd])
                eb3 = EB.rearrange("p (c t) -> p c t", t=64)
                S.op("dve", P(V.tensor_tensor, out=KlT.rearrange("p (c t) -> p c t", t=64), in0=KdT.rearrange("p (c t) -> p c t", t=64),
                              in1=eb3[:, :, 63:64].to_broadcast([128, 8, 64]), op=ALU.mult), reads=[tKd, tEB], writes=[tKl])

            def st5():
                klv = ps[0].bitcast(BF16).rearrange("p (a b) -> p a b", a=8)
                for sub in range(4):
                    S.op("pe", P(nc.tensor.transpose, klv[:, sub, :], KlT[:, sub * 128:(sub + 1) * 128], identb), reads=[tKl, tcb], writes=[ptok[0]])
                S.op("act", P(A, out=Kltm.rearrange("p (a b) -> p a b", a=4), in_=klv[:, 0:4, :], func=AF.Copy), reads=[ptok[0]], writes=[tKt])

            def st6():
                for sub in range(4):
                    mm(ps[4][:, sub * 128:(sub + 1) * 128], KdT[:, sub * 128:(sub + 1) * 128], QsT[:, sub * 128:(sub + 1) * 128],
                       sub == 0, sub == 3, [tKd, tQs], [ptok[4]])
                S.op("dve", P(V.tensor_tensor, out=ScT, in0=ps[4], in1=maskbd, op=ALU.mult), reads=[ptok[4], tcb], writes=[tSc])

            def st7():
                for ch in range(8):
                    sub = ch // 2
                    pb = (ch % 2) * 64
                    bank = 6 + (ch % 2)
                    mm(ps[bank][:, sub * 128:(sub + 1) * 128], Kltm[pb:pb + 64, sub * 128:(sub + 1) * 128], Vh[pb:pb + 64, sub, h * 128:(h + 1) * 128],
                       True, True, [tKt, tVh], [ptok[bank]])
                S.op("dve", P(V.tensor_copy, out=Sall[:, 0, :], in_=carry[:, h, :]), reads=[tcar[h]], writes=[tSa])
                for ch in range(8):
                    sub = ch // 2
                    bank = 6 + (ch % 2)
                    c0 = ch * 64
                    if ch < 7:
                        S.op("dve", P(V.scalar_tensor_tensor, out=Sall[:, ch + 1, :], in0=Sall[:, ch, :], scalar=EB[:, c0 + 63:c0 + 64],
                                      in1=ps[bank][:, sub * 128:(sub + 1) * 128], op0=ALU.mult, op1=ALU.add),
                             reads=[tSa, tEB, ptok[bank]], writes=[tSa])
                    else:
                        S.op("dve", P(V.scalar_tensor_tensor, out=carry[:, h, :], in0=Sall[:, ch, :], scalar=EB[:, c0 + 63:c0 + 64],
                                      in1=ps[bank][:, sub * 128:(sub + 1) * 128], op0=ALU.mult, op1=ALU.add),
                             reads=[tSa, tEB, ptok[bank]], writes=[tcar[h]])
                S.op("act", P(A, out=Sb, in_=Sall.rearrange("p a b -> p (a b)"), func=AF.Copy), reads=[tSa], writes=[tSb])

            def st8():
                for sub in range(4):
                    mm(ps[5][:, sub * 128:(sub + 1) * 128], Vh[:, sub, h * 128:(h + 1) * 128], ScT[:, sub * 128:(sub + 1) * 128],
                       sub == 0, False, [tVh, tSc], [ptok[5]])
                for ch in range(8):
                    c0 = ch * 64
                    mm(ps[5][:, c0:c0 + 64], Sb[:, ch * 128:(ch + 1) * 128], QsT[:, c0:c0 + 64], False, ch == 7, [tSb, tQs], [ptok[5]])
                S.op("act", P(A, out=Osb, in_=ps[5], func=AF.Copy), reads=[ptok[5]], writes=[tOsb])

            def st9():
                S.op("act", P(A, out=OSQ, in_=Osb, func=AF.Square), reads=[tOsb], writes=[tOS])
                mm(ps[1], onesb, OSQ, True, True, [tcb, tOS], [ptok[1]])
                S.op("act", P(A, out=RS, in_=ps[1], func=AF.Ln, scale=1.0 / 128, bias=epsc), reads=[ptok[1], tsm], writes=[tRS])
                S.op("act", P(A, out=RS, in_=RS, func=AF.Exp, scale=-0.5), reads=[tRS], writes=[tRS])
                S.op("act", P(A, out=GE, in_=Gg, func=AF.Exp, scale=-1.0), reads=[tGg], writes=[tGE])
                S.op("act", P(A, out=GE, in_=GE, func=AF.Ln, bias=onec), reads=[tGE, tsm], writes=[tGE])
                S.op("act", P(A, out=GE, in_=GE, func=AF.Exp, scale=-1.0), reads=[tGE], writes=[tGE])
                S.op("pool", P(nc.gpsimd.tensor_tensor, out=GE, in0=GE, in1=Gg, op=ALU.mult), reads=[tGE, tGg], writes=[tGE])
                S.op("dve", P(V.tensor_tensor, out=RS, in0=RS, in1=Osb, op=ALU.mult), reads=[tRS, tOsb], writes=[tRS])
                S.op("dve", P(V.scalar_tensor_tensor, out=OH, in0=RS, scalar=gnt[:, h:h + 1], in1=GE, op0=ALU.mult, op1=ALU.mult),
                     reads=[tRS, tGE, tvec], writes=[tOH])
                oh4 = OH.rearrange("p (q two t) -> p q two t", two=2, t=128)
                tb3 = TB[:, 0:256].rearrange("p (q t) -> p q t", t=128)
                S.op("dve", P(V.tensor_scalar, out=tb3, in0=oh4[:, :, 1, :], scalar1=msel, scalar2=None, op0=ALU.mult), reads=[tOH, tvec], writes=[tTB])
                dst = o_hg_own[:, h, T * 256:(T + 1) * 256].rearrange("p (q t) -> p q t", t=128)
                S.op("dve", P(V.scalar_tensor_tensor, out=dst, in0=oh4[:, :, 0, :], scalar=omsel, in1=tb3, op0=ALU.mult, op1=ALU.add),
                     reads=[tOH, tTB, tvec], writes=[t_ohg])

            return [st1, st2, st3, st4, st5, st6, st7, st8, st9]

        NT_B = {"b1": 1, "b2": 2, "b3": 3}.get(upto, 8)
        NSL = len(slots)
        LAG = 9 // NSL + (1 if 9 % NSL else 0)
        tls = {}
        items = [(T, h) for T in range(NT_B) for h in range(4)]
        stage_lists = {}
        nsteps = (len(items) - 1) * LAG + 9
        tls[0] = S0(0)
        for c in range(8):
            S.dma("pool", P(nc.gpsimd.dma_start, out=WA[:, c, :], in_=w_in[c * 128:(c + 1) * 128, 0:1536]), reads=[tls[0]["tVh"]], writes=[tWA[c]])
        for step in range(nsteps):
            for k in range(len(items)):
                s_i = step - k * LAG
                if s_i < 0 or s_i >= 9:
                    continue
                T, h = items[k]
                if k not in stage_lists:
                    if h == 2 and T + 1 < NT_B and (T + 1) not in tls:
                        tls[T + 1] = S0(T + 1)
                    stage_lists[k] = head_stages(T, h, slots[k % NSL], tls[T])
                stage_lists[k][s_i]()
                if s_i == 8:
                    del stage_lists[k]
        if upto in ("b1", "b2", "b3", "b"):
            dump(o_hg_own[:, 0, 0:256], t_ohg, 256)
            dump(o_hg_own[:, 3, 0:256], t_ohg, 256)
            if upto == "b":
                dump(o_hg_own[:, 1, 1792:2048], t_ohg, 256)
            if upto == "b3":
                dump(o_hg_own[:, 1, 256:768], t_ohg, 512)
            S.emit(ctx)
            build.stats = S.stats
            return nc
        S.barrier()

        st["off"] = base0
        st["lim"] = TOPA
        o_sbT = alloc([128, 4, S_OWN]); t_osb = Tok()
        baseT1 = st["off"]
        KT = alloc([128, 4, S_ALL]); tKT = [Tok() for _ in range(32)]
        Vsb = alloc([128, 32, 512]); tV = [Tok() for _ in range(32)]
        QT = alloc([128, 4, S_OWN]); tQT = [Tok() for _ in range(16)]
        baseC = st["off"]
        RA = norm_rings(4, 3)
        xnTr = Ring(3, [128, 8, 512])

        def normA(idx):
            xnT, txn = xnTr.next()
            src = x_all if idx < 8 else x_own
            T = idx if idx < 8 else idx - 8
            for sub in range(4):
                r0 = T * 512 + sub * 128
                norm_T(RA, src[r0:r0 + 128, :], True, None, g1t, xnT[:, :, sub * 128:(sub + 1) * 128], txn, 0)
            return xnT, txn

        nxt = normA(0)
        for idx in range(12):
            xnT, txn = nxt
            if idx + 1 < 12:
                nxt = normA(idx + 1)
            if idx < 8:
                T = idx
                for hp in range(4):
                    bank = 1 + (hp % 2)
                    for c in range(8):
                        mm(ps[bank], WA[:, c, 512 + hp * 128: 512 + (hp + 1) * 128], xnT[:, c, :], c == 0, c == 7, [txn, tWA[c]], [ptok[bank]])
                    S.op("act", P(A, out=KT[:, hp, T * 512:(T + 1) * 512], in_=ps[bank], func=AF.Copy),
                         reads=[ptok[bank]], writes=[tKT[T * 4 + k] for k in range(4)])
                for sub in range(4):
                    bank = 3 + (sub % 2)
                    for c in range(8):
                        mm(ps[bank], xnT[:, c, sub * 128:(sub + 1) * 128], WA[:, c, 1024:1536], c == 0, c == 7, [txn, tWA[c]], [ptok[bank]])
                    S.op("dve", P(V.tensor_copy, out=Vsb[:, T * 4 + sub, :], in_=ps[bank]), reads=[ptok[bank]], writes=[tV[T * 4 + sub]])
            else:
                T = idx - 8
                for hp in range(4):
                    bank = 1 + (hp % 2)
                    for c in range(8):
                        mm(ps[bank], WA[:, c, hp * 128:(hp + 1) * 128], xnT[:, c, :], c == 0, c == 7, [txn, tWA[c]], [ptok[bank]])
                    S.op("act", P(A, out=QT[:, hp, T * 512:(T + 1) * 512], in_=ps[bank], func=AF.Copy, scale=0.125),
                         reads=[ptok[bank]], writes=[tQT[T * 4 + k] for k in range(4)])
        S.barrier()

        TOPC = NB - 24576
        st["off"] = TOPC
        st["lim"] = NB
        WG = alloc([128, 8, 2048]); tWG = [Tok() for _ in range(8)]
        WOS = alloc([128, 4, D]); tWOS = Tok()
        WOH = alloc([128, 4, D]); tWOH = Tok()
        for c in range(8):
            S.dma("pool", P(nc.gpsimd.dma_start, out=WG[:, c, :], in_=w_in[c * 128:(c + 1) * 128, 3584:5632]), writes=[tWG[c]])
        S.dma("pool", P(nc.gpsimd.dma_start, out=WOS, in_=w_o_sb.rearrange("(c p) n -> p c n", p=128)), writes=[tWOS])
        S.dma("pool", P(nc.gpsimd.dma_start, out=WOH, in_=w_o_hg.rearrange("(c p) n -> p c n", p=128)), writes=[tWOH])
        st["off"] = baseC
        st["lim"] = TOPC
        Er = Ring(3, [128, 1024], F32)
        SPr = Ring(3, [128, 1024])
        Sfr = Ring(2, [128, 1024], F32)
        Sbr = Ring(3, [128, 1024])
        Wr = Ring(3, [128, 1024])
        zb = [(0, 1), (2, 3), (4, 5)]
        units = []
        NSLOT = 16
        for j in range(NSLOT):
            nb = 2 * j + 2
            for i in range(nb):
                units.append(dict(j=j, i=i, kb=2 * j + 1 - i, last=(i == nb - 1)))
        prevS = {}

        def stage0(u, k):
            b0, b1 = zb[k % 3]
            u["zb"] = (b0, b1)
            j, kb = u["j"], u["kb"]
            for h in range(8):
                bank = b0 if h % 2 == 0 else b1
                pb = (h % 2) * 64
                mm(ps[bank][:, (h // 2) * 128:(h // 2 + 1) * 128], KT[pb:pb + 64, h // 2, kb * 128:(kb + 1) * 128],
                   QT[pb:pb + 64, h // 2, j * 128:(j + 1) * 128], h // 2 == 0, False, [tKT[kb], tQT[j]], [ptok[bank]])
            if u["i"] < 2:
                M = mhi if u["i"] == 0 else mlo
                for n, bank in enumerate((b0, b1)):
                    mm(ps[bank], identb, M[:, n * 512:(n + 1) * 512], False, False, [tcb], [ptok[bank]])

        def stage1(u, k):
            b0, b1 = u["zb"]
            E, tE = Er.next()
            SP, tSP = SPr.next()
            u["SP"] = (SP, tSP)
            zp = pairs[b0 // 2][:, :]
            S.op("act", P(A, out=E, in_=zp, func=AF.Exp), reads=[ptok[b0], ptok[b1]], writes=[tE])
            S.op("act", P(A, out=SP, in_=E, func=AF.Ln, bias=onec), reads=[tE, tsm], writes=[tSP])

        def stage2(u, k):
            b0, b1 = u["zb"]
            SP, tSP = u["SP"]
            i = u["i"]
            for n, bank in enumerate((b0, b1)):
                mm(ps[bank], negtri, SP[:, n * 512:(n + 1) * 512], False, i == 0, [tcb, tSP], [ptok[bank]])
            if i > 0:
                Sb, tSb = prevS["bf"]
                for n, bank in enumerate((b0, b1)):
                    mm(ps[bank], negones, Sb[:, n * 512:(n + 1) * 512], False, True, [tcb, tSb], [ptok[bank]])
            if not u["last"]:
                if i == 0:
                    prevS["bf"] = (SP, tSP)
                    prevS["f32"] = (SP, tSP)
                else:
                    Sp, tSp = prevS["f32"]
                    Sn, tSn = Sfr.next()
                    Sbn, tSbn = Sbr.next()
                    S.op("dve", P(V.tensor_tensor, out=Sbn, in0=Sp, in1=SP, op=ALU.add), reads=[tSp, tSP], writes=[tSbn])
                    S.op("dve", P(V.tensor_tensor, out=Sn, in0=Sp, in1=SP, op=ALU.add), reads=[tSp, tSP], writes=[tSn])
                    prevS["f32"] = (Sn, tSn)
                    prevS["bf"] = (Sbn, tSbn)

        def stage3(u, k):
            b0, b1 = u["zb"]
            W, tW = Wr.next()
            u["W"] = (W, tW)
            zp = pairs[b0 // 2][:, :]
            S.op("act", P(A, out=W, in_=zp, func=AF.Exp), reads=[ptok[b0], ptok[b1]], writes=[tW])

        def stage4(u, k):
            W, tW = u["W"]
            j, kb, i = u["j"], u["kb"], u["i"]
            OB = 6 + (j % 2)
            for h in range(8):
                pb = (h % 2) * 64
                mm(ps[OB][pb:pb + 64, (h // 2) * 128:(h // 2 + 1) * 128], Vsb[:, kb, h * 64:(h + 1) * 64], W[:, (h % 2) * 512 + (h // 2) * 128:(h % 2) * 512 + (h // 2 + 1) * 128],
                   (i == 0 and h < 2), u["last"] and h >= 6, [tV[kb], tW], [ptok[OB]])
            if u["last"]:
                S.op("dve", P(V.tensor_copy, out=o_sbT[:, :, j * 128:(j + 1) * 128], in_=ps[OB].rearrange("p (a b) -> p a b", a=4)),
                     reads=[ptok[OB]], writes=[t_osb])

        stages = [stage0, stage1, stage2, stage3, stage4]
        lag = [0, 1, 1, 2, 2]
        NU = len(units)
        for t in range(NU + 3):
            for s_i, fn in enumerate(stages):
                k = t - lag[s_i]
                if 0 <= k < NU:
                    fn(units[k], k)
        if upto == "c":
            dump(o_sbT[:, 0, 0:256], t_osb, 256)
            dump(o_sbT[:, 3, 1792:2048], t_osb, 256)
            S.emit(ctx)
            build.stats = S.stats
            return nc
        S.barrier()

        st["off"] = baseT1
        st["lim"] = TOPC
        h_all = alloc([128, 16, D], F32); th = [Tok() for _ in range(16)]
        baseT2 = st["off"]
        WOUT = alloc([128, 8, D]); tWO = [Tok() for _ in range(8)]
        for c in range(8):
            S.dma("pool", P(nc.gpsimd.dma_start, out=WOUT[:, c, :], in_=w_out[c * 128:(c + 1) * 128, :]), writes=[tWO[c]])
        RT = {"x": Ring(2, [128, D], F32), "stat": Ring(3, [128, 4], F32), "xnb": Ring(1, [128, D])}
        RXres = Ring(1, [128, D], F32)
        xnTr = Ring(1, [128, 8, 512])
        gr = Ring(4, [128, 512], F32)
        mTr = Ring(1, [128, 8, 512])
        for T in range(4):
            xnT, txn = xnTr.next()
            xkeep = []
            for sub in range(4):
                r0 = T * 512 + sub * 128
                xb, tx, _, _ = norm_T(RT, x_own[r0:r0 + 128, :], True, None, g1t, xnT[:, :, sub * 128:(sub + 1) * 128], txn, 0, pool_rstd=True)
                xkeep.append((xb, tx))
            mT, tmT = mTr.next()
            for dc in range(8):
                pbs, pbh = (3, 4) if dc % 2 == 0 else (5, 6)
                for br in range(2):
                    bank = 1 + br
                    for c in range(8):
                        mm(ps[bank], WG[:, c, br * 1024 + dc * 128: br * 1024 + (dc + 1) * 128], xnT[:, c, :], c == 0, c == 7, [txn, tWG[c]], [ptok[bank]])
                for hp in range(4):
                    mm(ps[pbs], WOS[:, hp, dc * 128:(dc + 1) * 128], o_sbT[:, hp, T * 512:(T + 1) * 512], hp == 0, hp == 3, [tWOS, t_osb], [ptok[pbs]])
                for hh in range(4):
                    mm(ps[pbh], WOH[:, hh, dc * 128:(dc + 1) * 128], o_hg_own[:, hh, T * 512:(T + 1) * 512], hh == 0, hh == 3, [tWOH, t_ohg], [ptok[pbh]])
                gs = []
                for br in range(2):
                    Gt, tGt = gr.next()
                    bcol = pbg[:, br * 8 + dc: br * 8 + dc + 1]
                    S.op("act", P(A, out=Gt, in_=ps[1 + br], func=AF.Sigmoid, bias=bcol), reads=[ptok[1 + br], tvec], writes=[tGt])
                    gs.append((Gt, tGt))
                (Ga, tGa), (Gb, tGb) = gs
                S.op("dve", P(V.tensor_tensor, out=Ga, in0=Ga, in1=ps[pbs], op=ALU.mult), reads=[tGa, ptok[pbs]], writes=[tGa])
                S.op("dve", P(V.tensor_tensor, out=Gb, in0=Gb, in1=ps[pbh], op=ALU.mult), reads=[tGb, ptok[pbh]], writes=[tGb])
                S.op("dve", P(V.tensor_tensor, out=mT[:, dc, :], in0=Ga, in1=Gb, op=ALU.add), reads=[tGa, tGb], writes=[tmT])
            for sub in range(4):
                blk = T * 4 + sub
                xb, tx = RXres.next()
                S.dma("sp", P(nc.sync.dma_start, out=xb, in_=x_own[blk * 128:(blk + 1) * 128, :]), writes=[tx])
                for n in range(2):
                    bank = (7, 3)[n]
                    for dc in range(8):
                        mm(ps[bank], mT[:, dc, sub * 128:(sub + 1) * 128], WOUT[:, dc, n * 512:(n + 1) * 512], dc == 0, dc == 7, [tmT, tWO[dc]], [ptok[bank]])
                    S.op("dve", P(V.tensor_tensor, out=h_all[:, blk, n * 512:(n + 1) * 512], in0=ps[bank],
                                                                                    in1=xb[:, n * 512:(n + 1) * 512], op=ALU.add),
                         reads=[ptok[bank], tx], writes=[th[blk]])
        S.barrier()

        st["off"] = baseT2
        st["lim"] = NB
        hnT = alloc([128, 8, S_OWN]); thn = [Tok() for _ in range(16)]
        fgb = alloc([128, D], F32); tfg = Tok()
        S.dma("sp", P(nc.sync.dma_start, out=fgb, in_=fgb_d), writes=[tfg])
        W1r = Ring(2, [128, 8, 512])
        W2r = Ring(2, [128, 4, 1024])
        RN = {"stat": Ring(3, [128, 4], F32), "xnb": Ring(2, [128, D])}
        aTr = Ring(2, [128, 4, 512])
        rlr = Ring(3, [128, 512], F32)
        yr = Ring(2, [128, D], F32)
        wbuf = []
        for k in range(2):
            W1, _ = W1r.next(); W2, _ = W2r.next()
            wbuf.append((W1, W2, [Tok() for _ in range(8)], [Tok() for _ in range(4)]))

        def load_q(q):
            W1, W2, t1s, t2s = wbuf[q % 2]
            for c in range(8):
                S.dma("pool", P(nc.gpsimd.dma_start, out=W1[:, c, :], in_=w_ff1[c * 128:(c + 1) * 128, q * 512:(q + 1) * 512]), writes=[t1s[c]])
            for c in range(4):
                r0 = q * 512 + c * 128
                S.dma("pool", P(nc.gpsimd.dma_start, out=W2[:, c, :], in_=w_ff2[r0:r0 + 128, :]), writes=[t2s[c]])
        load_q(0)
        load_q(1)
        for blk in range(16):
            norm_T(RN, h_all[:, blk, :], False, th[blk], g2t, hnT[:, :, blk * 128:(blk + 1) * 128], thn[blk], 0)
        NQ = 8
        for q in range(NQ):
            W1, W2, t1s, t2s = wbuf[q % 2]
            for T in range(4):
                aT, taT = aTr.next()
                for fc in range(4):
                    bank = 1 + (fc % 2)
                    for c in range(8):
                        mm(ps[bank], W1[:, c, fc * 128:(fc + 1) * 128], hnT[:, c, T * 512:(T + 1) * 512], c == 0, c == 7,
                           [t1s[c]] + thn[T * 4:(T + 1) * 4], [ptok[bank]])
                    RL, tRL = rlr.next()
                    S.op("act", P(A, out=RL, in_=ps[bank], func=AF.Relu), reads=[ptok[bank]], writes=[tRL])
                    S.op("dve", P(V.tensor_tensor, out=aT[:, fc, :], in0=RL, in1=RL, op=ALU.mult), reads=[tRL], writes=[taT])
                for sub in range(4):
                    blk = T * 4 + sub
                    for n in range(2):
                        bank = 3 + n + 2 * (sub % 2)
                        for fc in range(4):
                            mm(ps[bank], aT[:, fc, sub * 128:(sub + 1) * 128], W2[:, fc, n * 512:(n + 1) * 512], fc == 0, fc == 3, [taT, t2s[fc]], [ptok[bank]])
                        S.op("dve", P(V.tensor_tensor, out=h_all[:, blk, n * 512:(n + 1) * 512], in0=ps[bank],
                                                                                 in1=h_all[:, blk, n * 512:(n + 1) * 512], op=ALU.add),
                             reads=[ptok[bank], th[blk]], writes=[th[blk]])
                    if q == NQ - 1:
                        junk, tj = RN["xnb"].next()
                        sm, ts = RN["stat"].next()
                        S.op("pool", P(nc.gpsimd.memset, sm, 0.0), writes=[ts])
                        S.op("act", P(A, out=junk, in_=h_all[:, blk, :], func=AF.Square, accum_out=sm[:, 0:1]), reads=[th[blk], ts], writes=[tj, ts])
                        S.op("act", P(A, out=sm[:, 1:2], in_=sm[:, 0:1], func=AF.Ln, scale=1.0 / D, bias=epsc), reads=[ts, tsm], writes=[ts])
                        S.op("act", P(A, out=sm[:, 2:3], in_=sm[:, 1:2], func=AF.Exp, scale=-0.5), reads=[ts], writes=[ts])
                        Y, tY = yr.next()
                        S.op("dve", P(V.scalar_tensor_tensor, out=Y, in0=h_all[:, blk, :], scalar=sm[:, 2:3], in1=fgb, op0=ALU.mult, op1=ALU.mult),
                             reads=[th[blk], ts, tfg], writes=[tY])
                        S.dma("sp", P(nc.sync.dma_start, out=y_own[blk * 128:(blk + 1) * 128, :], in_=Y), reads=[tY], is_out=True)
            if q + 2 < NQ:
                load_q(q + 2)
        S.emit(ctx)
        build.stats = S.stats
    return nc


def host_inputs(inputs):
    bf = ml_dtypes.bfloat16
    x = np.asarray(inputs["x"], np.float32)
    f32 = lambda k: np.ascontiguousarray(np.asarray(inputs[k], np.float32))
    w_in = f32("w_in")[0]; w_o_sb = f32("w_o_sb")[0]; w_o_hg = f32("w_o_hg")[0]; w_out = f32("w_out")[0]
    w_ff1 = f32("w_ff1")[0]; w_ff2 = f32("w_ff2")[0]
    g1 = f32("norm1_g")[0]; g2 = f32("norm2_g")[0]; bg = f32("b_gate")[0]; lbl = f32("lb_logits"); gn = f32("hg_norm_g")[0]
    fg = f32("final_g")
    cb = np.zeros((128, 3072), np.float32)
    cb[:, 0:128] = np.eye(128)
    jj = np.arange(128)[:, None]; kk = np.arange(128)[None, :]
    cb[:, 128:256] = -((jj >= kk).astype(np.float32))
    cb[:, 256:384] = -1.0
    cb[:, 384:512] = 1.0
    bd = ((jj // 64) == (kk // 64)) & (jj <= kk)
    cb[:, 512:1024] = np.tile(bd.astype(np.float32), (1, 4))
    diag = np.where(jj < kk, 0.0, NEG)
    scanm = np.ones((128, 512), np.float32); scanm[:, ::64] = 0.0
    maps = []
    for c in range(8):
        b, a = c // 2, c % 2
        vec = np.zeros((128, 64), np.float32)
        vec[:, 0:8] = g1.reshape(8, 128).T
        vec[:, 8:16] = g2.reshape(8, 128).T
        vec[:, 16:32] = -bg.reshape(16, 128).T
        vec[:, 32:36] = lbl[0].reshape(4, 128).T
        vec[:, 36:40] = lbl[1].reshape(4, 128).T
        vec[:, 40:44] = gn.reshape(4, 128).T
        vec[:, 44] = float(a); vec[:, 45] = 1.0 - a
        vec[:, 46:62] = bg.reshape(16, 128).T
        cbc = cb.copy()
        if a == 1:
            cbc[:, 1024:2048] = np.tile(diag, (1, 8)); cbc[:, 2048:3072] = 0.0
        else:
            cbc[:, 1024:2048] = NEG; cbc[:, 2048:3072] = np.tile(diag, (1, 8))
        xb = x[b]
        xo = np.ascontiguousarray(xb.reshape(16, 2, 128, D)[:, a].reshape(S_OWN, D))
        maps.append({"x_all": np.ascontiguousarray(xb), "x_own": xo, "w_in": w_in, "w_o_sb": w_o_sb, "w_o_hg": w_o_hg, "w_out": w_out,
                     "w_ff1": w_ff1, "w_ff2": w_ff2, "vecs": vec, "fgb": np.ascontiguousarray(np.broadcast_to(fg[None, :], (128, D))),
                     "cbf": cbc.astype(bf), "scanm": scanm})
    return maps


def kernel(**inputs):
    maps = host_inputs(inputs)
    nc = build()
    res = run_bass_kernel_spmd(nc, maps, core_ids=list(range(8)))
    out = np.zeros((4, S_ALL, D), np.float32)
    for c in range(8):
        b, a = c // 2, c % 2
        y = np.asarray(res.results[c]["y_own"], np.float32).reshape(16, 128, D)
        out[b].reshape(16, 2, 128, D)[:, a] = y
    return out
```

```python
import numpy as np
from contextlib import ExitStack
from functools import partial as P
import ml_dtypes
import concourse.bass as bass
import concourse.mybir as mybir
from concourse.bass_utils import run_bass_kernel_spmd

F32 = mybir.dt.float32
BF16 = mybir.dt.bfloat16
AF = mybir.ActivationFunctionType
ALU = mybir.AluOpType

D = 1024
S_ALL = 4096
S_OWN = 2048
NEG = -30000.0
EPS = 1e-6
import os
NOREUSEWAIT = bool(int(os.environ.get("NOREUSEWAIT", "0")))
REORDER = bool(int(os.environ.get("REORDER", "1")))
STRICT = bool(int(os.environ.get("STRICT", "1")))
ALPHA = float(os.environ.get("ALPHA", "0.05"))


class Tok:
    __slots__ = ("name", "writer", "readers")

    def __init__(self, name=""):
        self.name = name
        self.writer = None
        self.readers = {}


class Op:
    __slots__ = ("eng", "fn", "deps", "needs_inc", "cnt", "is_dma", "sem", "val", "_f", "snap", "cost", "idx", "fin")

    def __init__(self, eng, fn, is_dma):
        self.eng = eng
        self.fn = fn
        self.is_dma = is_dma
        self.deps = []
        self.needs_inc = False
        self.cnt = 0
        self.sem = None
        self.val = 0
        self._f = []
        self.snap = None
        self.cost = 0.5
        self.idx = 0
        self.fin = 0.0


class Sched:
    RING = {"sp": 16, "pool": 12}
    CENG = ("pe", "act", "dve", "pool")

    def __init__(self, nc):
        self.nc = nc
        self.h = {"pe": nc.tensor, "act": nc.scalar, "dve": nc.vector, "pool": nc.gpsimd, "sp": nc.sync}
        self.ops = []
        self.out_dmas = []

    def _deps(self, o, reads, writes):
        deps = {}
        for t in reads:
            if t.writer is not None:
                deps[id(t.writer)] = (t.writer, "raw")
        for t in writes:
            if t.writer is not None and id(t.writer) not in deps:
                deps[id(t.writer)] = (t.writer, "waw")
            for k, r in t.readers.items():
                rs = r if isinstance(r, list) else [r]
                for rr in rs:
                    if id(rr) not in deps:
                        deps[id(rr)] = (rr, "war")
        for t in reads:
            t.readers.setdefault(o.eng, [])
            lst = t.readers[o.eng]
            if len(lst) < 64:
                lst.append(o)
            else:
                if id(lst[-1]) not in deps:
                    deps[id(lst[-1])] = (lst[-1], "ord")
                lst[:] = [o]
        for t in writes:
            t.writer = o
            t.readers = {}
        deps.pop(id(o), None)
        o.deps = list(deps.values())

    @staticmethod
    def _nfree(fn):
        ap = None
        if hasattr(fn, "keywords") and "out" in fn.keywords:
            ap = fn.keywords["out"]
        elif hasattr(fn, "args") and fn.args:
            ap = fn.args[0]
        try:
            sh = ap.shape
            n = 1
            for d in sh[1:]:
                n *= int(d)
            return n, ap
        except Exception:
            return 512, None

    def op(self, eng, fn, reads=(), writes=(), cost=None):
        o = Op(eng, fn, False)
        self._deps(o, reads, writes)
        if cost is None:
            n, ap = self._nfree(fn)
            if eng == "pe":
                rhs = fn.args[2] if hasattr(fn, "args") and len(fn.args) > 2 else None
                try:
                    n = 1
                    for d in rhs.shape[1:]:
                        n *= int(d)
                except Exception:
                    pass
                cost = max(n, 64) / 2400.0 + 0.01
            elif eng == "act":
                cost = 0.12 + n / 1200.0
            elif eng == "dve":
                cost = 0.12 + n / 960.0
            else:
                cost = 0.2 + n / 420.0
        o.cost = cost
        o.idx = len(self.ops)
        self.ops.append(o)
        return o

    def dma(self, queue, fn, reads=(), writes=(), is_out=False):
        o = Op(queue, fn, True)
        self._deps(o, reads, writes)
        n, ap = self._nfree(fn)
        o.cost = 2.0 + n * 128 * 4 / 150e3
        o.idx = len(self.ops)
        self.ops.append(o)
        if is_out:
            self.out_dmas.append(o)
        return o

    def _reorder(self):
        import heapq
        LAT = 0.2
        segs = [[]]
        for o in self.ops:
            if o.eng == "bar":
                segs.append(o)
                segs.append([])
            else:
                segs[-1].append(o)
        new_ops = []
        tnow = 0.0
        for seg in segs:
            if not isinstance(seg, list):
                new_ops.append(seg)
                continue
            inseg = {id(o) for o in seg}
            indeg = {}
            succ = {}
            ready = {}
            for o in seg:
                k = 0
                for d, kind in o.deps:
                    if id(d) in inseg:
                        k += 1
                        succ.setdefault(id(d), []).append(o)
                indeg[id(o)] = k
                ready[id(o)] = tnow
            tail = {}
            for o in reversed(seg):
                t = 0.0
                for sc in succ.get(id(o), ()):
                    t = max(t, tail[id(sc)] + LAT)
                tail[id(o)] = t + min(o.cost, 3.0)
            heap = [(tnow - ALPHA * tail[id(o)], o.idx, o, tnow) for o in seg if indeg[id(o)] == 0]
            heapq.heapify(heap)
            free = {}
            tend = tnow
            while heap:
                _, _, o, r = heapq.heappop(heap)
                st_t = max(r, free.get(o.eng, tnow))
                if o.is_dma:
                    free[o.eng] = st_t + (0.8 if o.eng == "pool" else 0.15)
                    fin = st_t + o.cost
                else:
                    fin = st_t + o.cost
                    free[o.eng] = fin
                o.fin = fin
                tend = max(tend, fin)
                new_ops.append(o)
                for sc in succ.get(id(o), ()):
                    ready[id(sc)] = max(ready[id(sc)], fin + LAT)
                    indeg[id(sc)] -= 1
                    if indeg[id(sc)] == 0:
                        heapq.heappush(heap, (ready[id(sc)] - ALPHA * tail[id(sc)], sc.idx, sc, ready[id(sc)]))
            tnow = tend
        assert len(new_ops) == len(self.ops)
        self.ops = new_ops
        self.sim_us = tnow

    def barrier(self):
        o = Op("bar", None, False)
        o.idx = len(self.ops)
        self.ops.append(o)

    def _filtered(self, o):
        res = []
        for d, kind in o.deps:
            if d.is_dma or o.is_dma:
                res.append(d)
            elif d.eng == o.eng:
                if (kind == "raw" or STRICT) and o.eng != "pe":
                    res.append(d)
            else:
                res.append(d)
        return res

    def emit(self, ctx):
        nc = self.nc
        if REORDER:
            self._reorder()
        last = {}
        for pos, o in enumerate(self.ops):
            o.idx = pos
        for o in self.ops:
            if o.eng == "bar":
                for e, lo in last.items():
                    lo.needs_inc = True
                continue
            f = self._filtered(o)
            best = {}
            keep = []
            for d in f:
                if d.is_dma:
                    keep.append(d)
                elif d.eng not in best or best[d.eng].idx < d.idx:
                    best[d.eng] = d
            o._f = keep + list(best.values())
            for d in o._f:
                if not d.is_dma:
                    d.needs_inc = True
            if not o.is_dma:
                last[o.eng] = o
        esem = {e: ctx.enter_context(nc.semaphore("s_" + e)) for e in self.CENG}
        bsem = ctx.enter_context(nc.semaphore("s_bar"))
        rings = {q: [ctx.enter_context(nc.semaphore("r_%s%d" % (q, i))) for i in range(n)] for q, n in self.RING.items()}
        cnt = {e: 0 for e in esem}
        dcnt = {q: 0 for q in rings}
        rval = {}
        for o in self.ops:
            if o.eng == "bar":
                o.snap = (dict(cnt), dict(rval))
                continue
            if o.is_dma:
                k = dcnt[o.eng]
                dcnt[o.eng] = k + 1
                R = len(rings[o.eng])
                o.sem = rings[o.eng][k % R]
                o.val = 16 * (k // R + 1)
                rval[(o.eng, k % R)] = o.val
            else:
                if o.needs_inc:
                    cnt[o.eng] += 1
                o.cnt = cnt[o.eng]
                o.sem = esem[o.eng]
                o.val = o.cnt
        seen = {}
        nwait = 0
        nbar = 0
        for o in self.ops:
            if o.eng == "bar":
                c, rv = o.snap
                sp = self.h["sp"]
                for e, v in c.items():
                    if v > 0 and seen.get(("sp", id(esem[e])), 0) < v:
                        sp.wait_ge(esem[e], v)
                for (q, i), v in rv.items():
                    if seen.get(("sp", id(rings[q][i])), 0) < v:
                        sp.wait_ge(rings[q][i], v)
                nbar += 1
                sp.nop().then_inc(bsem, 1)
                for e in self.CENG:
                    self.h[e].wait_ge(bsem, nbar)
                for F in list(self.CENG) + ["sp"]:
                    for e, v in c.items():
                        seen[(F, id(esem[e]))] = max(seen.get((F, id(esem[e])), 0), v)
                    for (q, i), v in rv.items():
                        seen[(F, id(rings[q][i]))] = max(seen.get((F, id(rings[q][i])), 0), v)
                continue
            hdl = self.h[o.eng]
            waits = {}
            for d in o._f:
                key = id(d.sem)
                if key not in waits or waits[key][1] < d.val:
                    waits[key] = (d.sem, d.val)
            if o.is_dma and o.val > 16 and not NOREUSEWAIT:
                key = id(o.sem)
                v = o.val - 16
                if key not in waits or waits[key][1] < v:
                    waits[key] = (o.sem, v)
            selfwait = False
            for key, (sem, val) in waits.items():
                sk = (o.eng, key)
                if seen.get(sk, 0) >= val:
                    continue
                seen[sk] = val
                hdl.wait_ge(sem, val)
                nwait += 1
                if o.is_dma:
                    selfwait = True
            if selfwait:
                hdl.nop(nofuse=True)
            ins = o.fn()
            if o.is_dma:
                ins.then_inc(o.sem, 16)
            elif o.needs_inc:
                ins.then_inc(o.sem, 1)
        hdl = self.h["sp"]
        fin = {}
        for o in self.out_dmas:
            key = id(o.sem)
            if key not in fin or fin[key][1] < o.val:
                fin[key] = (o.sem, o.val)
        for key, (sem, val) in fin.items():
            if seen.get(("sp", key), 0) >= val:
                continue
            hdl.wait_ge(sem, val)
        self.stats = dict(nops=len(self.ops), nwait=nwait, cnt=cnt, dcnt=dcnt)


def build(dbg=0, upto="all"):
    nc = bass.Bass("TRN2", target_bir_lowering=False)

    def din(name, shape, dt=F32):
        return nc.dram_tensor(name, list(shape), dt, kind="ExternalInput").ap()

    x_all = din("x_all", [S_ALL, D])
    x_own = din("x_own", [S_OWN, D])
    w_in = din("w_in", [D, 5632])
    w_o_sb = din("w_o_sb", [512, D])
    w_o_hg = din("w_o_hg", [512, D])
    w_out = din("w_out", [D, D])
    w_ff1 = din("w_ff1", [D, 4096])
    w_ff2 = din("w_ff2", [4096, D])
    vecs_d = din("vecs", [128, 64])
    fgb_d = din("fgb", [128, D])
    cb_d = din("cbf", [128, 3072], BF16)
    scanm_d = din("scanm", [128, 512])
    y_own = nc.dram_tensor("y_own", [S_OWN, D], F32, kind="ExternalOutput").ap()
    dbg_out = nc.dram_tensor("dbg", [128, dbg], F32, kind="ExternalOutput").ap() if dbg else None

    ctx = ExitStack()
    with ctx:
        NB = 106400
        big = ctx.enter_context(nc.sbuf_tensor("big", [128, NB], BF16))
        pairs = [ctx.enter_context(nc.psum_tensor("pp%d" % i, [128, 1024], F32)) for i in range(4)]
        ps = [pairs[i // 2][:, (i % 2) * 512:(i % 2 + 1) * 512] for i in range(8)]
        ptok = [Tok("ps%d" % i) for i in range(8)]
        S = Sched(nc)
        A = nc.scalar.activation
        V = nc.vector
        st = {"off": 0}

        def alloc(shape, dt=BF16):
            n = int(np.prod(shape[1:]))
            nb = n * (2 if dt == F32 else 1)
            nbp = (nb + 15) // 16 * 16
            assert st["off"] + nbp <= st.get("lim", NB), ("sbuf overflow", st["off"], nbp, st.get("lim", NB))
            v = big[:, st["off"]:st["off"] + nb]
            st["off"] += nbp
            if dt == F32:
                v = v.bitcast(F32)
            if len(shape) == 3:
                v = v.rearrange("p (a b) -> p a b", a=shape[1])
            return v

        class Ring:
            def __init__(self, n, shape, dt=BF16):
                self.bufs = [(alloc(shape, dt), Tok()) for _ in range(n)]
                self.i = 0

            def next(self):
                b = self.bufs[self.i % len(self.bufs)]
                self.i += 1
                return b

        def mm(out, lhsT, rhs, start, stop, reads, writes):
            S.op("pe", P(nc.tensor.matmul, out, lhsT, rhs, start=start, stop=stop, skip_group_check=True), reads=reads, writes=writes)

        dstate = {"col": 0}

        def dump(ap, tok, n):
            if dbg_out is None:
                return
            c0 = dstate["col"]
            dstate["col"] += n
            assert dstate["col"] <= dbg
            tmpd = alloc([128, n], F32)
            td = Tok()
            S.op("dve", P(V.tensor_copy, out=tmpd, in_=ap), reads=[tok], writes=[td])
            S.dma("sp", P(nc.sync.dma_start, out=dbg_out[:, c0:c0 + n], in_=tmpd), reads=[td], is_out=True)

        vecs = alloc([128, 64], F32); tvec = Tok()
        cbf = alloc([128, 3072]); tcb = Tok()
        small = alloc([128, 32], F32); tsm = Tok()
        S.dma("sp", P(nc.sync.dma_start, out=vecs, in_=vecs_d), writes=[tvec])
        S.dma("sp", P(nc.sync.dma_start, out=cbf, in_=cb_d), writes=[tcb])
        identb = cbf[:, 0:128]; negtri = cbf[:, 128:256]; negones = cbf[:, 256:384]; onesb = cbf[:, 384:512]
        maskbd = cbf[:, 512:1024]; mhi = cbf[:, 1024:2048]; mlo = cbf[:, 2048:3072]
        g1t = vecs[:, 0:8]; g2t = vecs[:, 8:16]; nbg = vecs[:, 16:32]; lbl = vecs[:, 32:40]; gnt = vecs[:, 40:44]
        msel = vecs[:, 44:45]; omsel = vecs[:, 45:46]; pbg = vecs[:, 46:62]
        onec = small[:, 0:1]; epsc = small[:, 1:2]; lbv = small[:, 2:6]; lnoml = small[:, 6:10]; tmp4 = small[:, 10:14]
        S.op("pool", P(nc.gpsimd.memset, small, 0.0), writes=[tsm])
        S.op("pool", P(nc.gpsimd.memset, onec, 1.0), reads=[tsm], writes=[tsm])
        S.op("pool", P(nc.gpsimd.memset, epsc, EPS), reads=[tsm], writes=[tsm])
        mhalf = small[:, 14:15]
        S.op("pool", P(nc.gpsimd.memset, mhalf, -0.5), reads=[tsm], writes=[tsm])
        S.op("dve", P(V.tensor_tensor, out=tmp4, in0=lbl[:, 4:8], in1=lbl[:, 0:4], op=ALU.subtract), reads=[tvec, tsm], writes=[tsm])
        S.op("act", P(A, out=tmp4, in_=tmp4, func=AF.Exp), reads=[tsm], writes=[tsm])
        S.op("dve", P(V.tensor_scalar, out=tmp4, in0=tmp4, scalar1=1.0, scalar2=None, op0=ALU.add), reads=[tsm], writes=[tsm])
        S.op("dve", P(V.reciprocal, out=lbv, in_=tmp4), reads=[tsm], writes=[tsm])
        S.op("dve", P(V.tensor_scalar, out=tmp4, in0=lbv, scalar1=-1.0, scalar2=1.0, op0=ALU.mult, op1=ALU.add), reads=[tsm], writes=[tsm])
        S.op("act", P(A, out=lnoml, in_=tmp4, func=AF.Ln), reads=[tsm], writes=[tsm])

        o_hg_own = alloc([128, 4, S_OWN]); t_ohg = Tok()
        base0 = st["off"]

        def norm_T(R, src, src_is_dram, src_tok, gt, dst, dst_tok, xt_bank, pool_rstd=False):
            if src_is_dram:
                xb, tx = R["x"].next()
                S.dma("sp", P(nc.sync.dma_start, out=xb, in_=src), writes=[tx])
            else:
                xb, tx = src, src_tok
            xnb, tn = R["xnb"].next()
            sm, ts = R["stat"].next()
            S.op("pool", P(nc.gpsimd.memset, sm, 0.0), writes=[ts])
            S.op("act", P(A, out=xnb, in_=xb, func=AF.Square, accum_out=sm[:, 0:1]), reads=[tx, ts], writes=[tn, ts])
            if pool_rstd:
                S.op("dve", P(V.tensor_scalar, out=sm[:, 1:2], in0=sm[:, 0:1], scalar1=1.0 / D, scalar2=EPS, op0=ALU.mult, op1=ALU.add), reads=[ts], writes=[ts])
                S.op("pool", P(nc.gpsimd.tensor_tensor, out=sm[:, 2:3], in0=sm[:, 1:2], in1=mhalf, op=ALU.pow), reads=[ts, tsm], writes=[ts])
            else:
                S.op("act", P(A, out=sm[:, 1:2], in_=sm[:, 0:1], func=AF.Ln, scale=1.0 / D, bias=epsc), reads=[ts, tsm], writes=[ts])
                S.op("act", P(A, out=sm[:, 2:3], in_=sm[:, 1:2], func=AF.Exp, scale=-0.5), reads=[ts], writes=[ts])
            S.op("dve", P(V.tensor_scalar, out=xnb, in0=xb, scalar1=sm[:, 2:3], scalar2=None, op0=ALU.mult), reads=[tx, ts], writes=[tn])
            xtv = ps[xt_bank].bitcast(BF16).rearrange("p (a b) -> p a b", a=8)
            for c in range(8):
                S.op("pe", P(nc.tensor.transpose, xtv[:, c, :], xnb[:, c * 128:(c + 1) * 128], identb),
                     reads=[tn, tcb], writes=[ptok[xt_bank]])
            S.op("dve", P(V.tensor_tensor, out=dst, in0=xtv, in1=gt.unsqueeze(2).to_broadcast([128, 8, 128]), op=ALU.mult),
                 reads=[ptok[xt_bank], tvec], writes=[dst_tok])
            return xb, tx, sm, ts

        def norm_rings(nx=3, nxnb=2):
            return {"x": Ring(nx, [128, D], F32), "stat": Ring(3, [128, 4], F32), "xnb": Ring(nxnb, [128, D])}

        st["off"] = base0
        WB = alloc([128, 8, 2048]); tWB = [Tok() for _ in range(8)]
        for c in range(8):
            S.dma("pool", P(nc.gpsimd.dma_start, out=WB[:, c, :], in_=w_in[c * 128:(c + 1) * 128, 1536:3584]), writes=[tWB[c]])
        scanm = alloc([128, 512], F32); tscan = Tok()
        S.dma("sp", P(nc.sync.dma_start, out=scanm, in_=scanm_d), writes=[tscan])
        keep_off = st["off"]
        TOPA = NB - 12288
        st["off"] = TOPA
        WA = alloc([128, 8, 1536]); tWA = [Tok() for _ in range(8)]
        st["off"] = keep_off
        st["lim"] = TOPA
        RB = norm_rings(3, 2)
        xnTr = Ring(2, [128, 8, 512])
        Vh_r = Ring(2, [128, 4, 512])
        carry = alloc([128, 4, 128], F32); tcar = [Tok() for _ in range(4)]
        for h in range(4):
            S.op("pool", P(nc.gpsimd.memset, carry[:, h, :], 0.0), writes=[tcar[h]])
        FN = ["F", "Q", "Gg", "E", "Aa", "BC", "QE", "EB", "T3", "RS", "GE", "Osb"]
        BN = ["QsT", "KdT", "KlT", "Kltm", "ScT", "OSQ", "OH", "TB"]
        slots = []
        for s_i in range(2):
            sl = {}
            for nm in FN:
                sl[nm] = (alloc([128, 512], F32), Tok())
            for nm in BN:
                sl[nm] = (alloc([128, 512]), Tok())
            sl["Sall"] = (alloc([128, 8, 128], F32), Tok())
            sl["Sb"] = (alloc([128, 1024]), Tok())
            slots.append(sl)
        pring = {"i": 0}
        build.memB = st["off"]

        def S0(T):
            xnT, txn = xnTr.next()
            for sub in range(4):
                r0 = T * 512 + sub * 128
                norm_T(RB, x_all[r0:r0 + 128, :], True, None, g1t, xnT[:, :, sub * 128:(sub + 1) * 128], txn, 0)
            Vh, tVh = Vh_r.next()
            for sub in range(4):
                for c in range(8):
                    mm(ps[1], xnT[:, c, sub * 128:(sub + 1) * 128], WB[:, c, 512:1024], c == 0, c == 7, [txn, tWB[c]], [ptok[1]])
                S.op("act", P(A, out=Vh[:, sub, :], in_=ps[1], func=AF.Copy), reads=[ptok[1]], writes=[tVh])
            return dict(xnT=xnT, txn=txn, Vh=Vh, tVh=tVh)

        def head_stages(T, h, sl, tl):
            xnT, txn, Vh, tVh = tl["xnT"], tl["txn"], tl["Vh"], tl["tVh"]
            g = lambda nm: sl[nm]
            (F, tF), (Q, tQ), (Gg, tGg), (E, tE), (Aa, tA), (BC, tBC) = g("F"), g("Q"), g("Gg"), g("E"), g("Aa"), g("BC")
            (QE, tQE), (EB, tEB), (T3, tT3), (RS, tRS), (GE, tGE), (Osb, tOsb) = g("QE"), g("EB"), g("T3"), g("RS"), g("GE"), g("Osb")
            (QsT, tQs), (KdT, tKd), (KlT, tKl), (Kltm, tKt), (ScT, tSc), (OSQ, tOS), (OH, tOH), (TB, tTB) = [g(n) for n in BN]
            (Sall, tSa), (Sb, tSb) = g("Sall"), g("Sb")

            def st1():
                for (dst, tdst, col0) in ((F, tF, 0), (Q, tQ, 1024), (Gg, tGg, 1536)):
                    bank = 2 + (pring["i"] % 2)
                    pring["i"] += 1
                    for c in range(8):
                        mm(ps[bank], WB[:, c, col0 + h * 128: col0 + (h + 1) * 128], xnT[:, c, :], c == 0, c == 7, [txn, tWB[c]], [ptok[bank]])
                    if dst is F:
                        S.op("dve", P(V.tensor_copy, out=dst, in_=ps[bank]), reads=[ptok[bank]], writes=[tdst])
                    else:
                        S.op("act", P(A, out=dst, in_=ps[bank], func=AF.Copy), reads=[ptok[bank]], writes=[tdst])

            def st2():
                S.op("act", P(A, out=E, in_=F, func=AF.Exp, scale=-1.0), reads=[tF], writes=[tE])
                S.op("act", P(A, out=Aa, in_=E, func=AF.Ln, scale=lbv[:, h:h + 1], bias=onec), reads=[tE, tsm], writes=[tA])
                S.op("act", P(A, out=E, in_=E, func=AF.Ln, bias=onec), reads=[tE, tsm], writes=[tE])
                S.op("pool", P(nc.gpsimd.tensor_tensor, out=Aa, in0=Aa, in1=E, op=ALU.subtract), reads=[tA, tE], writes=[tA])
                S.op("dve", P(V.tensor_tensor_scan, out=BC, data0=scanm, data1=Aa, initial=0.0, op0=ALU.mult, op1=ALU.add),
                     reads=[tscan, tA], writes=[tBC])

            def st3():
                S.op("act", P(A, out=QE, in_=Q, func=AF.Exp, scale=-1.0), reads=[tQ], writes=[tQE])
                S.op("act", P(A, out=QE, in_=QE, func=AF.Ln, bias=onec), reads=[tQE, tsm], writes=[tQE])
                S.op("act", P(A, out=QE, in_=QE, func=AF.Exp, scale=-1.0), reads=[tQE], writes=[tQE])
                S.op("act", P(A, out=EB, in_=BC, func=AF.Exp), reads=[tBC], writes=[tEB])
                S.op("pool", P(nc.gpsimd.tensor_tensor, out=QE, in0=QE, in1=Q, op=ALU.mult), reads=[tQE, tQ], writes=[tQE])
                S.op("dve", P(V.tensor_tensor, out=QsT, in0=QE, in1=EB, op=ALU.mult), reads=[tQE, tEB], writes=[tQs])

            def st4():
                S.op("dve", P(V.tensor_tensor, out=E, in0=E, in1=BC, op=ALU.add), reads=[tE, tBC], writes=[tE])
                S.op("dve", P(V.scalar_tensor_tensor, out=F, in0=F, scalar=-1.0, in1=E, op0=ALU.mult, op1=ALU.subtract),
                     reads=[tF, tE], writes=[tF])
                S.op("act", P(A, out=KdT, in_=F, func=AF.Exp, bias=lnoml[:, h:h + 1]), reads=[tF, tsm], writes=[tKd])
                eb3 = EB.rearrange("p (c t) -> p c t", t=64)
                S.op("dve", P(V.tensor_tensor, out=KlT.rearrange("p (c t) -> p c t", t=64), in0=KdT.rearrange("p (c t) -> p c t", t=64),
                              in1=eb3[:, :, 63:64].to_broadcast([128, 8, 64]), op=ALU.mult), reads=[tKd, tEB], writes=[tKl])

            def st5():
                klv = ps[0].bitcast(BF16).rearrange("p (a b) -> p a b", a=8)
                for sub in range(4):
                    S.op("pe", P(nc.tensor.transpose, klv[:, sub, :], KlT[:, sub * 128:(sub + 1) * 128], identb), reads=[tKl, tcb], writes=[ptok[0]])
                S.op("act", P(A, out=Kltm.rearrange("p (a b) -> p a b", a=4), in_=klv[:, 0:4, :], func=AF.Copy), reads=[ptok[0]], writes=[tKt])

            def st6():
                for sub in range(4):
                    mm(ps[4][:, sub * 128:(sub + 1) * 128], KdT[:, sub * 128:(sub + 1) * 128], QsT[:, sub * 128:(sub + 1) * 128],
                       sub == 0, sub == 3, [tKd, tQs], [ptok[4]])
                S.op("dve", P(V.tensor_tensor, out=ScT, in0=ps[4], in1=maskbd, op=ALU.mult), reads=[ptok[4], tcb], writes=[tSc])

            def st7():
                for ch in range(8):
                    sub = ch // 2
                    pb = (ch % 2) * 64
                    bank = 6 + (ch % 2)
                    mm(ps[bank][:, sub * 128:(sub + 1) * 128], Kltm[pb:pb + 64, sub * 128:(sub + 1) * 128], Vh[pb:pb + 64, sub, h * 128:(h + 1) * 128],
                       True, True, [tKt, tVh], [ptok[bank]])
                S.op("dve", P(V.tensor_copy, out=Sall[:, 0, :], in_=carry[:, h, :]), reads=[tcar[h]], writes=[tSa])
                for ch in range(8):
                    sub = ch // 2
                    bank = 6 + (ch % 2)
                    c0 = ch * 64
                    if ch < 7:
                        S.op("dve", P(V.scalar_tensor_tensor, out=Sall[:, ch + 1, :], in0=Sall[:, ch, :], scalar=EB[:, c0 + 63:c0 + 64],
                                      in1=ps[bank][:, sub * 128:(sub + 1) * 128], op0=ALU.mult, op1=ALU.add),
                             reads=[tSa, tEB, ptok[bank]], writes=[tSa])
                    else:
                        S.op("dve", P(V.scalar_tensor_tensor, out=carry[:, h, :], in0=Sall[:, ch, :], scalar=EB[:, c0 + 63:c0 + 64],
                                      in1=ps[bank][:, sub * 128:(sub + 1) * 128], op0=ALU.mult, op1=ALU.add),
                             reads=[tSa, tEB, ptok[bank]], writes=[tcar[h]])
                S.op("act", P(A, out=Sb, in_=Sall.rearrange("p a b -> p (a b)"), func=AF.Copy), reads=[tSa], writes=[tSb])

            def st8():
                for sub in range(4):
                    mm(ps[5][:, sub * 128:(sub + 1) * 128], Vh[:, sub, h * 128:(h + 1) * 128], ScT[:, sub * 128:(sub + 1) * 128],
                       sub == 0, False, [tVh, tSc], [ptok[5]])
                for ch in range(8):
                    c0 = ch * 64
                    mm(ps[5][:, c0:c0 + 64], Sb[:, ch * 128:(ch + 1) * 128], QsT[:, c0:c0 + 64], False, ch == 7, [tSb, tQs], [ptok[5]])
                S.op("act", P(A, out=Osb, in_=ps[5], func=AF.Copy), reads=[ptok[5]], writes=[tOsb])

            def st9():
                S.op("act", P(A, out=OSQ, in_=Osb, func=AF.Square), reads=[tOsb], writes=[tOS])
                mm(ps[1], onesb, OSQ, True, True, [tcb, tOS], [ptok[1]])
                S.op("act", P(A, out=RS, in_=ps[1], func=AF.Ln, scale=1.0 / 128, bias=epsc), reads=[ptok[1], tsm], writes=[tRS])
                S.op("act", P(A, out=RS, in_=RS, func=AF.Exp, scale=-0.5), reads=[tRS], writes=[tRS])
                S.op("act", P(A, out=GE, in_=Gg, func=AF.Exp, scale=-1.0), reads=[tGg], writes=[tGE])
                S.op("act", P(A, out=GE, in_=GE, func=AF.Ln, bias=onec), reads=[tGE, tsm], writes=[tGE])
                S.op("act", P(A, out=GE, in_=GE, func=AF.Exp, scale=-1.0), reads=[tGE], writes=[tGE])
                S.op("pool", P(nc.gpsimd.tensor_tensor, out=GE, in0=GE, in1=Gg, op=ALU.mult), reads=[tGE, tGg], writes=[tGE])
                S.op("dve", P(V.tensor_tensor, out=RS, in0=RS, in1=Osb, op=ALU.mult), reads=[tRS, tOsb], writes=[tRS])
                S.op("dve", P(V.scalar_tensor_tensor, out=OH, in0=RS, scalar=gnt[:, h:h + 1], in1=GE, op0=ALU.mult, op1=ALU.mult),
                     reads=[tRS, tGE, tvec], writes=[tOH])
                oh4 = OH.rearrange("p (q two t) -> p q two t", two=2, t=128)
                tb3 = TB[:, 0:256].rearrange("p (q t) -> p q t", t=128)
                S.op("dve", P(V.tensor_scalar, out=tb3, in0=oh4[:, :, 1, :], scalar1=msel, scalar2=None, op0=ALU.mult), reads=[tOH, tvec], writes=[tTB])
                dst = o_hg_own[:, h, T * 256:(T + 1) * 256].rearrange("p (q t) -> p q t", t=128)
                S.op("dve", P(V.scalar_tensor_tensor, out=dst, in0=oh4[:, :, 0, :], scalar=omsel, in1=tb3, op0=ALU.mult, op1=ALU.add),
                     reads=[tOH, tTB, tvec], writes=[t_ohg])

            return [st1, st2, st3, st4, st5, st6, st7, st8, st9]

        NT_B = {"b1": 1, "b2": 2, "b3": 3}.get(upto, 8)
        NSL = len(slots)
        LAG = 9 // NSL + (1 if 9 % NSL else 0)
        tls = {}
        items = [(T, h) for T in range(NT_B) for h in range(4)]
        stage_lists = {}
        nsteps = (len(items) - 1) * LAG + 9
        tls[0] = S0(0)
        for c in range(8):
            S.dma("pool", P(nc.gpsimd.dma_start, out=WA[:, c, :], in_=w_in[c * 128:(c + 1) * 128, 0:1536]), reads=[tls[0]["tVh"]], writes=[tWA[c]])
        for step in range(nsteps):
            for k in range(len(items)):
                s_i = step - k * LAG
                if s_i < 0 or s_i >= 9:
                    continue
                T, h = items[k]
                if k not in stage_lists:
                    if h == 2 and T + 1 < NT_B and (T + 1) not in tls:
                        tls[T + 1] = S0(T + 1)
                    stage_lists[k] = head_stages(T, h, slots[k % NSL], tls[T])
                stage_lists[k][s_i]()
                if s_i == 8:
                    del stage_lists[k]
        if upto in ("b1", "b2", "b3", "b"):
            dump(o_hg_own[:, 0, 0:256], t_ohg, 256)
            dump(o_hg_own[:, 3, 0:256], t_ohg, 256)
            if upto == "b":
                dump(o_hg_own[:, 1, 1792:2048], t_ohg, 256)
            if upto == "b3":
                dump(o_hg_own[:, 1, 256:768], t_ohg, 512)
            S.emit(ctx)
            build.stats = S.stats
            return nc
        S.barrier()

        st["off"] = base0
        st["lim"] = TOPA
        o_sbT = alloc([128, 4, S_OWN]); t_osb = Tok()
        baseT1 = st["off"]
        KT = alloc([128, 4, S_ALL]); tKT = [Tok() for _ in range(32)]
        Vsb = alloc([128, 32, 512]); tV = [Tok() for _ in range(32)]
        QT = alloc([128, 4, S_OWN]); tQT = [Tok() for _ in range(16)]
        baseC = st["off"]
        RA = norm_rings(4, 3)
        xnTr = Ring(3, [128, 8, 512])

        def normA(idx):
            xnT, txn = xnTr.next()
            src = x_all if idx < 8 else x_own
            T = idx if idx < 8 else idx - 8
            for sub in range(4):
                r0 = T * 512 + sub * 128
                norm_T(RA, src[r0:r0 + 128, :], True, None, g1t, xnT[:, :, sub * 128:(sub + 1) * 128], txn, 0)
            return xnT, txn

        nxt = normA(0)
        for idx in range(12):
            xnT, txn = nxt
            if idx + 1 < 12:
                nxt = normA(idx + 1)
            if idx < 8:
                T = idx
                for hp in range(4):
                    bank = 1 + (hp % 2)
                    for c in range(8):
                        mm(ps[bank], WA[:, c, 512 + hp * 128: 512 + (hp + 1) * 128], xnT[:, c, :], c == 0, c == 7, [txn, tWA[c]], [ptok[bank]])
                    S.op("act", P(A, out=KT[:, hp, T * 512:(T + 1) * 512], in_=ps[bank], func=AF.Copy),
                         reads=[ptok[bank]], writes=[tKT[T * 4 + k] for k in range(4)])
                for sub in range(4):
                    bank = 3 + (sub % 2)
                    for c in range(8):
                        mm(ps[bank], xnT[:, c, sub * 128:(sub + 1) * 128], WA[:, c, 1024:1536], c == 0, c == 7, [txn, tWA[c]], [ptok[bank]])
                    S.op("dve", P(V.tensor_copy, out=Vsb[:, T * 4 + sub, :], in_=ps[bank]), reads=[ptok[bank]], writes=[tV[T * 4 + sub]])
            else:
                T = idx - 8
                for hp in range(4):
                    bank = 1 + (hp % 2)
                    for c in range(8):
                        mm(ps[bank], WA[:, c, hp * 128:(hp + 1) * 128], xnT[:, c, :], c == 0, c == 7, [txn, tWA[c]], [ptok[bank]])
                    S.op("act", P(A, out=QT[:, hp, T * 512:(T + 1) * 512], in_=ps[bank], func=AF.Copy, scale=0.125),
                         reads=[ptok[bank]], writes=[tQT[T * 4 + k] for k in range(4)])
        S.barrier()

        TOPC = NB - 24576
        st["off"] = TOPC
        st["lim"] = NB
        WG = alloc([128, 8, 2048]); tWG = [Tok() for _ in range(8)]
        WOS = alloc([128, 4, D]); tWOS = Tok()
        WOH = alloc([128, 4, D]); tWOH = Tok()
        for c in range(8):
            S.dma("pool", P(nc.gpsimd.dma_start, out=WG[:, c, :], in_=w_in[c * 128:(c + 1) * 128, 3584:5632]), writes=[tWG[c]])
        S.dma("pool", P(nc.gpsimd.dma_start, out=WOS, in_=w_o_sb.rearrange("(c p) n -> p c n", p=128)), writes=[tWOS])
        S.dma("pool", P(nc.gpsimd.dma_start, out=WOH, in_=w_o_hg.rearrange("(c p) n -> p c n", p=128)), writes=[tWOH])
        st["off"] = baseC
        st["lim"] = TOPC
        Er = Ring(3, [128, 1024], F32)
        SPr = Ring(3, [128, 1024])
        Sfr = Ring(2, [128, 1024], F32)
        Sbr = Ring(3, [128, 1024])
        Wr = Ring(3, [128, 1024])
        zb = [(0, 1), (2, 3), (4, 5)]
        units = []
        NSLOT = 16
        for j in range(NSLOT):
            nb = 2 * j + 2
            for i in range(nb):
                units.append(dict(j=j, i=i, kb=2 * j + 1 - i, last=(i == nb - 1)))
        prevS = {}

        def stage0(u, k):
            b0, b1 = zb[k % 3]
            u["zb"] = (b0, b1)
            j, kb = u["j"], u["kb"]
            for h in range(8):
                bank = b0 if h % 2 == 0 else b1
                pb = (h % 2) * 64
                mm(ps[bank][:, (h // 2) * 128:(h // 2 + 1) * 128], KT[pb:pb + 64, h // 2, kb * 128:(kb + 1) * 128],
                   QT[pb:pb + 64, h // 2, j * 128:(j + 1) * 128], h // 2 == 0, False, [tKT[kb], tQT[j]], [ptok[bank]])
            if u["i"] < 2:
                M = mhi if u["i"] == 0 else mlo
                for n, bank in enumerate((b0, b1)):
                    mm(ps[bank], identb, M[:, n * 512:(n + 1) * 512], False, False, [tcb], [ptok[bank]])

        def stage1(u, k):
            b0, b1 = u["zb"]
            E, tE = Er.next()
            SP, tSP = SPr.next()
            u["SP"] = (SP, tSP)
            zp = pairs[b0 // 2][:, :]
            S.op("act", P(A, out=E, in_=zp, func=AF.Exp), reads=[ptok[b0], ptok[b1]], writes=[tE])
            S.op("act", P(A, out=SP, in_=E, func=AF.Ln, bias=onec), reads=[tE, tsm], writes=[tSP])

        def stage2(u, k):
            b0, b1 = u["zb"]
            SP, tSP = u["SP"]
            i = u["i"]
            for n, bank in enumerate((b0, b1)):
                mm(ps[bank], negtri, SP[:, n * 512:(n + 1) * 512], False, i == 0, [tcb, tSP], [ptok[bank]])
            if i > 0:
                Sb, tSb = prevS["bf"]
                for n, bank in enumerate((b0, b1)):
                    mm(ps[bank], negones, Sb[:, n * 512:(n + 1) * 512], False, True, [tcb, tSb], [ptok[bank]])
            if not u["last"]:
                if i == 0:
                    prevS["bf"] = (SP, tSP)
                    prevS["f32"] = (SP, tSP)
                else:
                    Sp, tSp = prevS["f32"]
                    Sn, tSn = Sfr.next()
                    Sbn, tSbn = Sbr.next()
                    S.op("dve", P(V.tensor_tensor, out=Sbn, in0=Sp, in1=SP, op=ALU.add), reads=[tSp, tSP], writes=[tSbn])
                    S.op("dve", P(V.tensor_tensor, out=Sn, in0=Sp, in1=SP, op=ALU.add), reads=[tSp, tSP], writes=[tSn])
                    prevS["f32"] = (Sn, tSn)
                    prevS["bf"] = (Sbn, tSbn)

        def stage3(u, k):
            b0, b1 = u["zb"]
            W, tW = Wr.next()
            u["W"] = (W, tW)
            zp = pairs[b0 // 2][:, :]
            S.op("act", P(A, out=W, in_=zp, func=AF.Exp), reads=[ptok[b0], ptok[b1]], writes=[tW])

        def stage4(u, k):
            W, tW = u["W"]
            j, kb, i = u["j"], u["kb"], u["i"]
            OB = 6 + (j % 2)
            for h in range(8):
                pb = (h % 2) * 64
                mm(ps[OB][pb:pb + 64, (h // 2) * 128:(h // 2 + 1) * 128], Vsb[:, kb, h * 64:(h + 1) * 64], W[:, (h % 2) * 512 + (h // 2) * 128:(h % 2) * 512 + (h // 2 + 1) * 128],
                   (i == 0 and h < 2), u["last"] and h >= 6, [tV[kb], tW], [ptok[OB]])
            if u["last"]:
                S.op("dve", P(V.tensor_copy, out=o_sbT[:, :, j * 128:(j + 1) * 128], in_=ps[OB].rearrange("p (a b) -> p a b", a=4)),
                     reads=[ptok[OB]], writes=[t_osb])

        stages = [stage0, stage1, stage2, stage3, stage4]
        lag = [0, 1, 1, 2, 2]
        NU = len(units)
        for t in range(NU + 3):
            for s_i, fn in enumerate(stages):
                k = t - lag[s_i]
                if 0 <= k < NU:
                    fn(units[k], k)
        if upto == "c":
            dump(o_sbT[:, 0, 0:256], t_osb, 256)
            dump(o_sbT[:, 3, 1792:2048], t_osb, 256)
            S.emit(ctx)
            build.stats = S.stats
            return nc
        S.barrier()

        st["off"] = baseT1
        st["lim"] = TOPC
        h_all = alloc([128, 16, D], F32); th = [Tok() for _ in range(16)]
        baseT2 = st["off"]
        WOUT = alloc([128, 8, D]); tWO = [Tok() for _ in range(8)]
        RT = {"x": Ring(2, [128, D], F32), "stat": Ring(3, [128, 4], F32), "xnb": Ring(1, [128, D])}
        RXres = Ring(1, [128, D], F32)
        xnTr = Ring(1, [128, 8, 512])
        gr = Ring(4, [128, 512], F32)
        mTr = Ring(1, [128, 8, 512])
        for T in range(4):
            xnT, txn = xnTr.next()
            xkeep = []
            for sub in range(4):
                r0 = T * 512 + sub * 128
                xb, tx, _, _ = norm_T(RT, x_own[r0:r0 + 128, :], True, None, g1t, xnT[:, :, sub * 128:(sub + 1) * 128], txn, 0, pool_rstd=True)
                xkeep.append((xb, tx))
            if T == 0:
                for c in range(8):
                    S.dma("pool", P(nc.gpsimd.dma_start, out=WOUT[:, c, :], in_=w_out[c * 128:(c + 1) * 128, :]), reads=[txn], writes=[tWO[c]])
            mT, tmT = mTr.next()
            for dc in range(8):
                pbs, pbh = (3, 4) if dc % 2 == 0 else (5, 6)
                for br in range(2):
                    bank = 1 + br
                    for c in range(8):
                        mm(ps[bank], WG[:, c, br * 1024 + dc * 128: br * 1024 + (dc + 1) * 128], xnT[:, c, :], c == 0, c == 7, [txn, tWG[c]], [ptok[bank]])
                for hp in range(4):
                    mm(ps[pbs], WOS[:, hp, dc * 128:(dc + 1) * 128], o_sbT[:, hp, T * 512:(T + 1) * 512], hp == 0, hp == 3, [tWOS, t_osb], [ptok[pbs]])
                for hh in range(4):
                    mm(ps[pbh], WOH[:, hh, dc * 128:(dc + 1) * 128], o_hg_own[:, hh, T * 512:(T + 1) * 512], hh == 0, hh == 3, [tWOH, t_ohg], [ptok[pbh]])
                gs = []
                for br in range(2):
                    Gt, tGt = gr.next()
                    bcol = pbg[:, br * 8 + dc: br * 8 + dc + 1]
                    S.op("act", P(A, out=Gt, in_=ps[1 + br], func=AF.Sigmoid, bias=bcol), reads=[ptok[1 + br], tvec], writes=[tGt])
                    gs.append((Gt, tGt))
                (Ga, tGa), (Gb, tGb) = gs
                S.op("dve", P(V.tensor_tensor, out=Ga, in0=Ga, in1=ps[pbs], op=ALU.mult), reads=[tGa, ptok[pbs]], writes=[tGa])
                S.op("dve", P(V.tensor_tensor, out=Gb, in0=Gb, in1=ps[pbh], op=ALU.mult), reads=[tGb, ptok[pbh]], writes=[tGb])
                S.op("dve", P(V.tensor_tensor, out=mT[:, dc, :], in0=Ga, in1=Gb, op=ALU.add), reads=[tGa, tGb], writes=[tmT])
            for sub in range(4):
                blk = T * 4 + sub
                xb, tx = RXres.next()
                S.dma("sp", P(nc.sync.dma_start, out=xb, in_=x_own[blk * 128:(blk + 1) * 128, :]), writes=[tx])
                for n in range(2):
                    bank = (7, 3)[n]
                    for dc in range(8):
                        mm(ps[bank], mT[:, dc, sub * 128:(sub + 1) * 128], WOUT[:, dc, n * 512:(n + 1) * 512], dc == 0, dc == 7, [tmT, tWO[dc]], [ptok[bank]])
                    S.op("dve", P(V.tensor_tensor, out=h_all[:, blk, n * 512:(n + 1) * 512], in0=ps[bank],
                                                                                    in1=xb[:, n * 512:(n + 1) * 512], op=ALU.add),
                         reads=[ptok[bank], tx], writes=[th[blk]])
        S.barrier()

        st["off"] = baseT2
        st["lim"] = NB
        hnT = alloc([128, 8, S_OWN]); thn = [Tok() for _ in range(16)]
        fgb = alloc([128, D], F32); tfg = Tok()
        S.dma("sp", P(nc.sync.dma_start, out=fgb, in_=fgb_d), writes=[tfg])
        W1r = Ring(2, [128, 8, 512])
        W2r = Ring(2, [128, 4, 1024])
        RN = {"stat": Ring(3, [128, 4], F32), "xnb": Ring(2, [128, D])}
        aTr = Ring(2, [128, 4, 512])
        rlr = Ring(3, [128, 512], F32)
        yr = Ring(2, [128, D], F32)
        wbuf = []
        for k in range(2):
            W1, _ = W1r.next(); W2, _ = W2r.next()
            wbuf.append((W1, W2, [Tok() for _ in range(8)], [Tok() for _ in range(4)]))

        def load_q(q):
            W1, W2, t1s, t2s = wbuf[q % 2]
            for c in range(8):
                S.dma("pool", P(nc.gpsimd.dma_start, out=W1[:, c, :], in_=w_ff1[c * 128:(c + 1) * 128, q * 512:(q + 1) * 512]), writes=[t1s[c]])
            for c in range(4):
                r0 = q * 512 + c * 128
                S.dma("pool", P(nc.gpsimd.dma_start, out=W2[:, c, :], in_=w_ff2[r0:r0 + 128, :]), writes=[t2s[c]])
        load_q(0)
        load_q(1)
        for blk in range(16):
            norm_T(RN, h_all[:, blk, :], False, th[blk], g2t, hnT[:, :, blk * 128:(blk + 1) * 128], thn[blk], 0)
        NQ = 8
        for q in range(NQ):
            W1, W2, t1s, t2s = wbuf[q % 2]
            for T in range(4):
                aT, taT = aTr.next()
                for fc in range(4):
                    bank = 1 + (fc % 2)
                    for c in range(8):
                        mm(ps[bank], W1[:, c, fc * 128:(fc + 1) * 128], hnT[:, c, T * 512:(T + 1) * 512], c == 0, c == 7,
                           [t1s[c]] + thn[T * 4:(T + 1) * 4], [ptok[bank]])
                    RL, tRL = rlr.next()
                    S.op("act", P(A, out=RL, in_=ps[bank], func=AF.Relu), reads=[ptok[bank]], writes=[tRL])
                    S.op("dve", P(V.tensor_tensor, out=aT[:, fc, :], in0=RL, in1=RL, op=ALU.mult), reads=[tRL], writes=[taT])
                for sub in range(4):
                    blk = T * 4 + sub
                    for n in range(2):
                        bank = 3 + n + 2 * (sub % 2)
                        for fc in range(4):
                            mm(ps[bank], aT[:, fc, sub * 128:(sub + 1) * 128], W2[:, fc, n * 512:(n + 1) * 512], fc == 0, fc == 3, [taT, t2s[fc]], [ptok[bank]])
                        S.op("dve", P(V.tensor_tensor, out=h_all[:, blk, n * 512:(n + 1) * 512], in0=ps[bank],
                                                                                 in1=h_all[:, blk, n * 512:(n + 1) * 512], op=ALU.add),
                             reads=[ptok[bank], th[blk]], writes=[th[blk]])
                    if q == NQ - 1:
                        junk, tj = RN["xnb"].next()
                        sm, ts = RN["stat"].next()
                        S.op("pool", P(nc.gpsimd.memset, sm, 0.0), writes=[ts])
                        S.op("act", P(A, out=junk, in_=h_all[:, blk, :], func=AF.Square, accum_out=sm[:, 0:1]), reads=[th[blk], ts], writes=[tj, ts])
                        S.op("act", P(A, out=sm[:, 1:2], in_=sm[:, 0:1], func=AF.Ln, scale=1.0 / D, bias=epsc), reads=[ts, tsm], writes=[ts])
                        S.op("act", P(A, out=sm[:, 2:3], in_=sm[:, 1:2], func=AF.Exp, scale=-0.5), reads=[ts], writes=[ts])
                        Y, tY = yr.next()
                        S.op("dve", P(V.scalar_tensor_tensor, out=Y, in0=h_all[:, blk, :], scalar=sm[:, 2:3], in1=fgb, op0=ALU.mult, op1=ALU.mult),
                             reads=[th[blk], ts, tfg], writes=[tY])
                        S.dma("sp", P(nc.sync.dma_start, out=y_own[blk * 128:(blk + 1) * 128, :], in_=Y), reads=[tY], is_out=True)
            if q + 2 < NQ:
                load_q(q + 2)
        S.emit(ctx)
        build.stats = S.stats
    return nc


def host_inputs(inputs):
    bf = ml_dtypes.bfloat16
    x = np.asarray(inputs["x"], np.float32)
    f32 = lambda k: np.ascontiguousarray(np.asarray(inputs[k], np.float32))
    w_in = f32("w_in")[0]; w_o_sb = f32("w_o_sb")[0]; w_o_hg = f32("w_o_hg")[0]; w_out = f32("w_out")[0]
    w_ff1 = f32("w_ff1")[0]; w_ff2 = f32("w_ff2")[0]
    g1 = f32("norm1_g")[0]; g2 = f32("norm2_g")[0]; bg = f32("b_gate")[0]; lbl = f32("lb_logits"); gn = f32("hg_norm_g")[0]
    fg = f32("final_g")
    cb = np.zeros((128, 3072), np.float32)
    cb[:, 0:128] = np.eye(128)
    jj = np.arange(128)[:, None]; kk = np.arange(128)[None, :]
    cb[:, 128:256] = -((jj >= kk).astype(np.float32))
    cb[:, 256:384] = -1.0
    cb[:, 384:512] = 1.0
    bd = ((jj // 64) == (kk // 64)) & (jj <= kk)
    cb[:, 512:1024] = np.tile(bd.astype(np.float32), (1, 4))
    diag = np.where(jj < kk, 0.0, NEG)
    scanm = np.ones((128, 512), np.float32); scanm[:, ::64] = 0.0
    maps = []
    for c in range(8):
        b, a = c // 2, c % 2
        vec = np.zeros((128, 64), np.float32)
        vec[:, 0:8] = g1.reshape(8, 128).T
        vec[:, 8:16] = g2.reshape(8, 128).T
        vec[:, 16:32] = -bg.reshape(16, 128).T
        vec[:, 32:36] = lbl[0].reshape(4, 128).T
        vec[:, 36:40] = lbl[1].reshape(4, 128).T
        vec[:, 40:44] = gn.reshape(4, 128).T
        vec[:, 44] = float(a); vec[:, 45] = 1.0 - a
        vec[:, 46:62] = bg.reshape(16, 128).T
        cbc = cb.copy()
        if a == 1:
            cbc[:, 1024:2048] = np.tile(diag, (1, 8)); cbc[:, 2048:3072] = 0.0
        else:
            cbc[:, 1024:2048] = NEG; cbc[:, 2048:3072] = np.tile(diag, (1, 8))
        xb = x[b]
        xo = np.ascontiguousarray(xb.reshape(16, 2, 128, D)[:, a].reshape(S_OWN, D))
        maps.append({"x_all": np.ascontiguousarray(xb), "x_own": xo, "w_in": w_in, "w_o_sb": w_o_sb, "w_o_hg": w_o_hg, "w_out": w_out,
                     "w_ff1": w_ff1, "w_ff2": w_ff2, "vecs": vec, "fgb": np.ascontiguousarray(np.broadcast_to(fg[None, :], (128, D))),
                     "cbf": cbc.astype(bf), "scanm": scanm})
    return maps


def kernel(**inputs):
    maps = host_inputs(inputs)
    nc = build()
    res = run_bass_kernel_spmd(nc, maps, core_ids=list(range(8)))
    out = np.zeros((4, S_ALL, D), np.float32)
    for c in range(8):
        b, a = c // 2, c % 2
        y = np.asarray(res.results[c]["y_own"], np.float32).reshape(16, 128, D)
        out[b].reshape(16, 2, 128, D)[:, a] = y
    return out
```
